# Optimizing a Trainium2 kernel written in Bass

```python
import math
import jax, jax.numpy as jnp
from jax import lax
import numpy as np

D_MODEL = 2048
BATCH = 16
SEQ = 2048
DEPTH = 2

CHUNK = 64
Q_BLOCK = 128
N_A = DEPTH // 2
N_B = DEPTH - N_A
RWKV_HEAD = 64
RWKV_HEADS = D_MODEL // RWKV_HEAD
DECAY_LORA = max(32, int(round(1.8 * D_MODEL ** 0.5 / 32)) * 32)
AAA_LORA = max(32, int(round(1.8 * D_MODEL ** 0.5 / 32)) * 32)
GN_EPS = 64e-5
DIFF_HEADS = D_MODEL // 256
DIFF_HEAD_DIM = D_MODEL // DIFF_HEADS // 2
SUBLN_EPS = 1e-5
DEEPNORM_ALPHA = (2.0 * DEPTH) ** 0.25
DEEPNORM_BETA = (8.0 * DEPTH) ** -0.25
LN_EPS = 1e-5

kernel_name = "yoco_rwkv7_diffattn_deepnorm"


def _layernorm(x, g, b):
    xf = x.astype(jnp.float32)
    mu = jnp.mean(xf, -1, keepdims=True)
    xc = xf - mu
    var = jnp.mean(xc * xc, -1, keepdims=True)
    return (xc * lax.rsqrt(var + LN_EPS) * g + b).astype(x.dtype)


def _wkv7_scan(r, w, k, v, kk, a):
    B, T, H, N = r.shape

    def step(S, inp):
        r_t, w_t, k_t, v_t, kk_t, a_t = inp
        sa = jnp.einsum('bhvk,bhk->bhv', S, kk_t)
        S = (S * w_t[:, :, None, :]
             - sa[..., None] * (kk_t * a_t)[:, :, None, :]
             + v_t[..., None] * k_t[:, :, None, :])
        y = jnp.einsum('bhvk,bhk->bhv', S, r_t)
        return S, y

    xs = tuple(jnp.moveaxis(t.astype(jnp.float32), 1, 0) for t in (r, w, k, v, kk, a))
    S0 = jnp.zeros((B, H, N, N), jnp.float32)
    _, ys = lax.scan(step, S0, xs)
    return jnp.moveaxis(ys, 0, 1)


def _rwkv7_mixer(x, mu_proj, mu_lora, w_in, w0, w1, w2, a0, a1, a2,
                 k_k, k_a, r_k, gn_g, gn_b, w_out):
    B, T, C = x.shape
    H, N = RWKV_HEADS, RWKV_HEAD
    xx = jnp.pad(x, ((0, 0), (1, 0), (0, 0)))[:, :-1] - x
    xs = x[None] + xx[None] * mu_proj[:, None, None, :]
    proj = jnp.einsum('sbtd,sde->sbte', xs, w_in)
    r, k, v, g = proj[0], proj[1], proj[2], proj[3]
    xw = x + xx * mu_lora[0]
    xa = x + xx * mu_lora[1]
    w_log = -jax.nn.softplus(-(w0 + jnp.tanh(xw @ w1) @ w2)) - 0.5
    decay = jnp.exp(-jnp.exp(w_log.astype(jnp.float32)))
    a = jax.nn.sigmoid(a0 + (xa @ a1) @ a2)
    kk = (k * k_k).reshape(B, T, H, N).astype(jnp.float32)
    kk = kk / jnp.maximum(jnp.sqrt(jnp.sum(kk * kk, -1, keepdims=True)), 1e-12)
    k = k * (1.0 + (a - 1.0) * k_a)
    rh = r.reshape(B, T, H, N)
    kh = k.reshape(B, T, H, N)
    vh = v.reshape(B, T, H, N)
    y = _wkv7_scan(rh, decay.reshape(B, T, H, N), kh, vh, kk, a.reshape(B, T, H, N))
    mu = jnp.mean(y, -1, keepdims=True)
    yc = y - mu
    var = jnp.mean(yc * yc, -1, keepdims=True)
    y = (yc * lax.rsqrt(var + GN_EPS)).reshape(B, T, C) * gn_g + gn_b
    bonus = jnp.sum(rh.astype(jnp.float32) * kh.astype(jnp.float32) * r_k, -1, keepdims=True) * vh.astype(jnp.float32)
    y = (y + bonus.reshape(B, T, C)).astype(x.dtype)
    return (y * jax.nn.silu(g)) @ w_out


def _diff_attention(q, k, v, lam):
    B, H2, T, dh = q.shape
    H = H2 // 2
    slopes = jnp.repeat(2.0 ** (-(8.0 / H) * jnp.arange(1, H + 1, dtype=jnp.float32)), 2)
    scale = dh ** -0.5
    outs = []
    for i in range(T // Q_BLOCK):
        q0, q1 = i * Q_BLOCK, (i + 1) * Q_BLOCK
        qb = q[:, :, q0:q1]
        kb = k[:, :, :q1]
        vb = v[:, :, :q1]
        s = jnp.einsum('bmqd,bmkd->bmqk', qb, kb).astype(jnp.float32) * scale
        tq = jnp.arange(q0, q1)
        tk = jnp.arange(q1)
        dist = jnp.abs(tq[:, None] - tk[None, :]).astype(jnp.float32)
        allowed = (tk[None, :] // CHUNK) <= (tq[:, None] // CHUNK)
        s = jnp.where(allowed, s - slopes[:, None, None] * dist, -jnp.inf)
        p = jax.nn.softmax(s, axis=-1).reshape(B, H, 2, Q_BLOCK, q1)
        attn = p[:, :, 0] - lam * p[:, :, 1]
        outs.append(jnp.einsum('bhqk,bhkd->bhqd', attn.astype(vb.dtype), vb))
    return jnp.concatenate(outs, axis=2)


def _diff_mixer(x, k_sh, v_sh, w_qg, lam_p, subln_g, w_out, layer):
    B, T, C = x.shape
    H, dh = DIFF_HEADS, DIFF_HEAD_DIM
    qg = x @ w_qg
    q = qg[..., :C].reshape(B, T, 2 * H, dh).transpose(0, 2, 1, 3)
    g = qg[..., C:]
    lam_init = 0.8 - 0.6 * math.exp(-0.3 * layer)
    lam_f = lam_p.astype(jnp.float32)
    lam = (jnp.exp(jnp.sum(lam_f[0] * lam_f[1])) - jnp.exp(jnp.sum(lam_f[2] * lam_f[3])) + lam_init)
    o = _diff_attention(q, k_sh, v_sh, lam).astype(jnp.float32)
    o = o * lax.rsqrt(jnp.mean(o * o, -1, keepdims=True) + SUBLN_EPS) * subln_g * (1.0 - lam_init)
    o = o.transpose(0, 2, 1, 3).reshape(B, T, C).astype(x.dtype)
    return (o * jax.nn.silu(g)) @ w_out


def setup_inputs(seed: int = 0) -> dict:
    key = jax.random.key(seed)
    ks = jax.random.split(key, 32)
    C, H, N, dh = D_MODEL, RWKV_HEADS, RWKV_HEAD, DIFF_HEAD_DIM
    f32 = jnp.float32
    nrm = lambda k, shape, s: jax.random.normal(k, shape, f32) * s
    return {
        "x": jax.random.normal(ks[0], (BATCH, SEQ, C), f32),
        "a_mu_proj": jax.random.uniform(ks[1], (N_A, 4, C), f32),
        "a_mu_lora": jax.random.uniform(ks[2], (N_A, 2, C), f32),
        "a_w_in": nrm(ks[3], (N_A, 4, C, C), C ** -0.5),
        "a_w0": jax.random.uniform(ks[4], (N_A, C), f32, -6.0, 0.0),
        "a_w1": nrm(ks[5], (N_A, C, DECAY_LORA), C ** -0.5),
        "a_w2": nrm(ks[6], (N_A, DECAY_LORA, C), 0.1 * DECAY_LORA ** -0.5),
        "a_a0": nrm(ks[7], (N_A, C), 0.5),
        "a_a1": nrm(ks[8], (N_A, C, AAA_LORA), C ** -0.5),
        "a_a2": nrm(ks[9], (N_A, AAA_LORA, C), 0.3 * AAA_LORA ** -0.5),
        "a_k_k": 0.85 + nrm(ks[10], (N_A, C), 0.05),
        "a_k_a": 1.0 + nrm(ks[11], (N_A, C), 0.05),
        "a_r_k": -0.04 + nrm(ks[12], (N_A, H, N), 0.1),
        "a_gn_g": 1.0 + nrm(ks[13], (N_A, C), 0.05),
        "a_gn_b": nrm(ks[14], (N_A, C), 0.01),
        "a_w_out": nrm(ks[15], (N_A, C, C), C ** -0.5 * DEEPNORM_BETA),
        "w_k_shared": nrm(ks[16], (C, C), C ** -0.5),
        "w_v_shared": nrm(ks[17], (C, C), C ** -0.5),
        "b_w_qg": nrm(ks[18], (N_B, C, 2 * C), C ** -0.5),
        "b_lambda": nrm(ks[19], (N_B, 4, dh), 0.1),
        "b_subln_g": 1.0 + nrm(ks[20], (N_B, 2 * dh), 0.05),
        "b_w_out": nrm(ks[21], (N_B, C, C), C ** -0.5 * DEEPNORM_BETA),
        "ln_g": 1.0 + nrm(ks[22], (DEPTH, C), 0.05),
        "ln_b": nrm(ks[23], (DEPTH, C), 0.01),
    }


def reference(x, a_mu_proj, a_mu_lora, a_w_in, a_w0, a_w1, a_w2, a_a0, a_a1, a_a2,
              a_k_k, a_k_a, a_r_k, a_gn_g, a_gn_b, a_w_out, w_k_shared, w_v_shared,
              b_w_qg, b_lambda, b_subln_g, b_w_out, ln_g, ln_b):
    B, T, C = x.shape
    k_sh = None
    v_sh = None
    for layer in range(DEPTH):
        if layer < N_A:
            i = layer
            y = _rwkv7_mixer(x, a_mu_proj[i], a_mu_lora[i], a_w_in[i], a_w0[i], a_w1[i], a_w2[i],
                             a_a0[i], a_a1[i], a_a2[i], a_k_k[i], a_k_a[i], a_r_k[i],
                             a_gn_g[i], a_gn_b[i], a_w_out[i])
        else:
            j = layer - N_A
            y = _diff_mixer(x, k_sh, v_sh, b_w_qg[j], b_lambda[j], b_subln_g[j], b_w_out[j], layer)
        x = _layernorm(DEEPNORM_ALPHA * x + y, ln_g[layer], ln_b[layer])
        if layer == N_A - 1:
            k_sh = (x @ w_k_shared).reshape(B, T, 2 * DIFF_HEADS, DIFF_HEAD_DIM).transpose(0, 2, 1, 3)
            v_sh = (x @ w_v_shared).reshape(B, T, DIFF_HEADS, 2 * DIFF_HEAD_DIM).transpose(0, 2, 1, 3)
    return x
```

```python
import math
from collections import deque
import numpy as np
from contextlib import ExitStack
import concourse.bass as bass
import concourse.mybir as mybir
from concourse.bass_utils import run_bass_kernel_spmd

F32 = mybir.dt.float32
BF16 = mybir.dt.bfloat16
AF = mybir.ActivationFunctionType
ALU = mybir.AluOpType
AX = mybir.AxisListType

ENGS = ["pe", "dve", "act", "pool", "sp"]
C0 = 0.6065306597126334
GN_EPS = 64e-5
LN_EPS = 1e-5
SUBLN_EPS = 1e-5
ALPHA = 4.0 ** 0.25
LAM_INIT = 0.8 - 0.6 * math.exp(-0.3 * 1)
LORA = 96
L = 64
SAME_ENG_WINDOW = 12
PUMP_MIN = 4.0
FRONT_RATIO = 0.65


class Sched:
    def __init__(self, nc, stack, ndma=48):
        self.nc = nc
        self.streams = {e: [] for e in ENGS}
        self.esem = {e: stack.enter_context(nc.semaphore("es_" + e)) for e in ENGS}
        self.ecnt = {e: 0 for e in ENGS}
        self.seen = {e: {} for e in ENGS}
        self.dsem = [stack.enter_context(nc.semaphore("ds%d" % i)) for i in range(ndma)]
        self.dcnt = [0] * ndma
        self.dnext = {"pool": 0, "hw": ndma // 2}
        self.drange = {"pool": (0, ndma // 2), "hw": (ndma // 2, ndma)}
        self.W = {}
        self.R = {}
        self.ninst = 0
        self.dead = False
        self.defer = None
        self.inter = None
        self._acc = 0.0
        self._pumping = False
        self.guard = None
        import os as _os
        self.maxi = int(_os.environ["FRONT_MAXI"]) if "FRONT_MAXI" in _os.environ else None
        self.npump = 0

    def _need(self, eng, deps):
        need = {}
        for tok in deps:
            if tok is None:
                continue
            key, h, val, src, seq = tok
            if src == eng:
                if eng == "pe":
                    continue
                if self.ecnt[eng] - seq >= SAME_ENG_WINDOW:
                    continue
            if self.seen[eng].get(key, 0) >= val:
                continue
            if key not in need or need[key][1] < val:
                need[key] = (h, val)
        return need

    def _collect(self, reads, writes):
        deps = []
        for r in reads:
            deps.append(self.W.get(r))
            if r.startswith("bank"):
                deps.extend(self.R.get(r, {}).values())
        for w in writes:
            deps.append(self.W.get(w))
            deps.extend(self.R.get(w, {}).values())
        return deps

    def _emit_waits(self, eng, need):
        for key, (h, val) in need.items():
            self.seen[eng][key] = val
            self.streams[eng].append(lambda e, h=h, val=val: e.wait_ge(h, val))
            self.ninst += 1

    def _record(self, tok, reads, writes):
        for r in reads:
            self.R.setdefault(r, {})[tok[0]] = tok
        for w in writes:
            self.W[w] = tok
            self.R[w] = {}

    def pump(self, lst, n):
        d, self.defer = self.defer, None
        p, self._pumping = self._pumping, True
        for _ in range(n):
            if not lst:
                break
            it = lst.popleft()
            if it[0] == "op":
                self.op(*it[1:])
            else:
                self.dma(*it[1:])
        self.defer = d
        self._pumping = p

    def _tick(self, eng=None):
        if self.inter is None or self._pumping:
            return
        lst, ratio = self.inter
        self._acc += ratio
        if self._acc < 1.0 or not lst:
            return
        if self.guard is not None and len(lst) <= self.guard:
            return
        budget = self._acc
        n = 0
        in_pe_run = False
        while lst:
            it = lst[0]
            ieng = it[1]
            if ieng == eng and not in_pe_run:
                break
            if budget < 1.0 and not (in_pe_run and ieng in ("pe", "pool")):
                break
            if self.guard is not None and len(lst) <= self.guard:
                break
            self.pump(lst, 1)
            in_pe_run = ieng in ("pe", "pool")
            budget -= 1.0
            n += 1
        self._acc = budget

    def op(self, eng, fn, reads=(), writes=()):
        if self.dead:
            return None
        if self.defer is not None:
            self.defer.append(("op", eng, fn, tuple(reads), tuple(writes)))
            return None
        need = self._need(eng, self._collect(reads, writes))
        self._emit_waits(eng, need)
        self.ecnt[eng] += 1
        h = self.esem[eng]
        self.streams[eng].append(lambda e, fn=fn, h=h: fn(e).then_inc(h, 1))
        tok = ("e_" + eng, h, self.ecnt[eng], eng, self.ecnt[eng])
        self._record(tok, reads, writes)
        self.ninst += 1
        self._tick(eng)
        return tok

    def dma(self, eng, out, in_, reads=(), writes=()):
        if self.dead:
            return None
        if self.defer is not None:
            self.defer.append(("dma", eng, out, in_, tuple(reads), tuple(writes)))
            return None
        grp = "pool" if eng == "pool" else "hw"
        k = self.dnext[grp]
        lo, hi = self.drange[grp]
        self.dnext[grp] = lo + (k + 1 - lo) % (hi - lo)
        deps = self._collect(reads, writes)
        key = "d%d" % k
        if self.dcnt[k] > 0:
            deps.append((key, self.dsem[k], self.dcnt[k], "dma", 0))
        need = self._need(eng, deps)
        self._emit_waits(eng, need)
        self.dcnt[k] += 16
        h = self.dsem[k]
        self.streams[eng].append(
            lambda e, out=out, in_=in_, h=h: e.dma_start(out=out, in_=in_).then_inc(h, 16))
        tok = (key, h, self.dcnt[k], "dma", 0)
        self._record(tok, reads, writes)
        self.ninst += 1
        return tok

    def link(self, src, dst):
        toks = {}
        for k in src:
            t = self.W.get(k)
            if t is not None and (t[0] not in toks or toks[t[0]][2] < t[2]):
                toks[t[0]] = t
            for t in self.R.get(k, {}).values():
                if t[0] not in toks or toks[t[0]][2] < t[2]:
                    toks[t[0]] = t
        for k in dst:
            d = self.R.setdefault(k, {})
            for sk, t in toks.items():
                if sk not in d or d[sk][2] < t[2]:
                    d[sk] = t

    def wait_all(self, eng, regions):
        need = self._need(eng, [self.W.get(r) for r in regions])
        self._emit_waits(eng, need)

    def barrier(self):
        if self.dead:
            return
        toks = []
        for e in ENGS:
            if self.ecnt[e] > 0:
                toks.append(("e_" + e, self.esem[e], self.ecnt[e], "bar", 0))
        for k in range(len(self.dsem)):
            if self.dcnt[k] > 0:
                toks.append(("d%d" % k, self.dsem[k], self.dcnt[k], "dma", 0))
        for e in ENGS:
            need = self._need(e, [t for t in toks if t[0] != "e_" + e])
            self._emit_waits(e, need)
        self.W = {}
        self.R = {}

    def finish(self):
        with self.nc.Block() as block:
            @block.tensor
            def _(e):
                for f in self.streams["pe"]:
                    f(e)

            @block.vector
            def _(e):
                for f in self.streams["dve"]:
                    f(e)

            @block.scalar
            def _(e):
                for f in self.streams["act"]:
                    f(e)

            @block.gpsimd
            def _(e):
                for f in self.streams["pool"]:
                    f(e)

            @block.sync
            def _(e):
                for f in self.streams["sp"]:
                    f(e)


W_R, W_K, W_V, W_G, W_AOUT, W_BK, W_BV, W_BQ, W_BG, W_BOUT = range(10)
V_W0, V_A0, V_KK, V_KA, V_RK, V_GNG, V_GNB, V_LNG0, V_LNB0, V_LNG1, V_LNB1 = range(11)
NV = 11


class _Stop(Exception):
    pass


def build(C, T, NSEQ, TT, debug=False, stop=0):
    NC = C // 128
    HB = C // 256
    NT = T // TT
    NCH = TT // L
    NB = TT // 128
    NQB = T // 128
    nc = bass.Bass("TRN2", target_bir_lowering=False)

    def din(name, shape):
        return nc.dram_tensor(name, list(shape), F32, kind="ExternalInput").ap()

    xT = din("xT", [NSEQ, C, T])
    wall = din("wall", [10, NC, 128, NC * 128])
    w1l = din("w1l", [128, NC * LORA])
    a1l = din("a1l", [128, NC * LORA])
    lw2d = din("lw2d", [NC, LORA, 256])
    mud = din("mud", [128, 6 * NC])
    vecd = din("vecd", [128, NV * NC])
    lamd = din("lamd", [128, 512])
    subgd = din("subgd", [128, 256])
    identd = din("identd", [128, 128])
    m5d = din("m5d", [128, 3 * 4 * 64])
    idmd = din("idmd", [128, 4 * 64])
    bonesd = din("bonesd", [128, 128])
    onescd = din("onescd", [128, 128])
    rmaskd = din("rmaskd", [128, TT])
    ualid = din("ualid", [128, HB * 512])
    dalid = din("dalid", [128, HB * 512])
    slpd = din("slpd", [128, HB * 4])
    outT = nc.dram_tensor("outT", [NSEQ, C, T], F32, kind="ExternalOutput").ap()
    if debug:
        x1f = nc.dram_tensor("x1f", [NSEQ, C, T], F32, kind="ExternalOutput").ap()
    else:
        x1f = nc.dram_tensor("x1f", [NSEQ, C, T], F32).ap()
    dbg = nc.dram_tensor("dbg", [NSEQ, C, T], F32, kind="ExternalOutput").ap() if debug else None
    KTd = nc.dram_tensor("KTd", [NSEQ, NC, 128, T], BF16).ap()
    QTd = nc.dram_tensor("QTd", [NSEQ, NC, 128, T], BF16).ap()
    SGd = nc.dram_tensor("SGd", [NSEQ, NC, 128, T], BF16).ap()
    OGd = nc.dram_tensor("OGd", [NSEQ, NC, 128, T], BF16).ap()
    VTd = nc.dram_tensor("VTd", [NSEQ, NQB, 128, C], BF16).ap()

    def ck(k):
        if stop == k:
            S.dead = True

    with ExitStack() as st:
        S = Sched(nc, st)
        _body(nc, S, st, locals())
        S.dead = False
        S.barrier()
        S.finish()
    return nc


def _body(nc, S, st, env):
    globals().update({})
    (C, T, NSEQ, TT, debug, NC, HB, NT, NCH, NB, NQB, ck) = (env[k] for k in
        ("C", "T", "NSEQ", "TT", "debug", "NC", "HB", "NT", "NCH", "NB", "NQB", "ck"))
    (xT, wall, w1l, a1l, lw2d, mud, vecd, lamd, subgd, identd, m5d, idmd, bonesd, onescd, rmaskd,
     ualid, dalid, slpd, outT, x1f, dbg, KTd, QTd, SGd, OGd, VTd) = (env[k] for k in
        ("xT", "wall", "w1l", "a1l", "lw2d", "mud", "vecd", "lamd", "subgd", "identd", "m5d", "idmd",
         "bonesd", "onescd", "rmaskd", "ualid", "dalid", "slpd", "outT", "x1f", "dbg", "KTd", "QTd", "SGd", "OGd", "VTd"))
    if True:

        def sb(name, shape, dt, stack=st):
            return stack.enter_context(nc.sbuf_tensor(name, list(shape), dt))

        ident = sb("ident", [128, 128], BF16)
        m3 = sb("m3", [128, 3, 4, 64], BF16)
        idm4 = sb("idm4", [128, 4, 64], BF16)
        bones = sb("bones", [128, 128], BF16)
        onesc = sb("onesc", [128, 128], BF16)
        vec = sb("vec", [128, NV, NC], F32)
        omka = sb("omka", [128, NC], F32)
        for nm, t, d in [("ident", ident, identd), ("m3", m3[:].rearrange("p a b c -> p (a b c)"), m5d), ("idm4", idm4[:].rearrange("p a b -> p (a b)"), idmd),
                         ("bones", bones, bonesd), ("onesc", onesc, onescd)]:
            S.dma("pool", t if nm in ("m3", "idm4") else t[:], d, writes=[nm])
        S.dma("sp", vec[:].rearrange("p a b -> p (a b)"), vecd, writes=["vec"])
        S.op("dve", lambda e: e.tensor_scalar(out=omka[:], in0=vec[:, V_KA, :], scalar1=-1.0, scalar2=1.0,
                                               op0=ALU.mult, op1=ALU.add), reads=["vec"], writes=["omka"])
        negv = sb("negv", [128, 2, NC], F32)
        S.op("dve", lambda e: e.tensor_scalar(out=negv[:], in0=vec[:, V_W0:V_A0 + 1, :], scalar1=-1.0, scalar2=None,
                                               op0=ALU.mult), reads=["vec"], writes=["negv"])
        ck(1)

        banks = [st.enter_context(nc.psum_tensor("bank%d" % i, [128, 512], F32)) for i in range(8)]
        bk = ["bank%d" % i for i in range(8)]

        NSLOT = 5
        wslots = [sb("wslot%d" % i, [128, NC, 128], BF16) for i in range(NSLOT)]
        worder = []
        for s in range(NSEQ):
            for j in range(NT):
                worder.append(("w1", 0))
                worder.append(("a1", 0))
                for c in range(NC):
                    for m in (W_V, W_G, W_R, W_K):
                        worder.append((m, c))
                for c in range(NC):
                    worder.append((W_AOUT, c))
        for s in range(NSEQ):
            for j in range(NT):
                for m in (W_BK, W_BV, W_BQ, W_BG):
                    for c in range(NC):
                        worder.append((m, c))
        for s in range(NSEQ):
            for j in range(NT):
                for c in range(NC):
                    worder.append((W_BOUT, c))
        wstate = {"issued": 0, "used": 0}
        PF = 4

        def wissue_upto(n):
            while wstate["issued"] < min(n, len(worder)):
                i = wstate["issued"]
                m, c = worder[i]
                sl = i % NSLOT
                if m in ("w1", "a1"):
                    S.dma("pool", wslots[sl][:].rearrange("p a b -> p (a b)")[:, 0:NC * LORA], w1l if m == "w1" else a1l,
                          writes=["wslot%d" % sl])
                else:
                    S.dma("pool", wslots[sl][:].rearrange("p a b -> p (a b)"), wall[m, c], writes=["wslot%d" % sl])
                wstate["issued"] += 1

        def wnext(m, c):
            i = wstate["used"]
            assert worder[i] == (m, c), (worder[i], m, c)
            wissue_upto(i + 1 + PF)
            wstate["used"] += 1
            return wslots[i % NSLOT], "wslot%d" % (i % NSLOT)

        def fm_proj(pbank, pkey, wt, wkey, rhs_fn, rkey):
            for dc in range(NC):
                S.op("pe", lambda e, dc=dc: e.matmul(pbank, lhsT=wt[:, dc, :], rhs=rhs_fn(dc),
                                                      start=(dc == 0), stop=(dc == NC - 1)),
                     reads=[wkey, rkey], writes=[pkey])

        def outproj_ln(stk, tagp):
            Z = sb(tagp + "Z", [128, NC, TT], F32, stk)
            mk = lambda nm, n, dt: [(sb(tagp + nm + str(i), [128, TT], dt, stk), tagp + nm + str(i)) for i in range(n)]
            return dict(Z=Z, xst=mk("xs", 2, F32), zb=mk("zb", 2, BF16), zq=mk("zq", 2, BF16),
                        lt=mk("lt", 4, F32), ost=mk("os", 2, F32))

        def run_outproj_ln(bufs, wm, src, srckey, resid_dram, dst_dram, vg, vb, tagp, boff=0):
            Z = bufs["Z"]
            xst = [t for t, _ in bufs["xst"]]
            xstk = [k for _, k in bufs["xst"]]
            zb = [t for t, _ in bufs["zb"]]
            zbk = [k for _, k in bufs["zb"]]
            zq = [t for t, _ in bufs["zq"]]
            zqk = [k for _, k in bufs["zq"]]
            lt = [t for t, _ in bufs["lt"]]
            kn = [k for _, k in bufs["lt"]]
            ost = [t for t, _ in bufs["ost"]]
            ostk = [k for _, k in bufs["ost"]]
            zk = tagp + "Z"
            pend = []

            def stats_mm(c, i2):
                S.op("pe", lambda e: e.matmul(banks[boff + 2][:, 0:TT], lhsT=onesc[:], rhs=zb[i2][:],
                                              start=(c == 0), stop=(c == NC - 1)),
                     reads=["onesc", zbk[i2]], writes=[bk[boff + 2]])
                S.op("pe", lambda e: e.matmul(banks[boff + 3][:, 0:TT], lhsT=onesc[:], rhs=zq[i2][:],
                                              start=(c == 0), stop=(c == NC - 1)),
                     reads=["onesc", zqk[i2]], writes=[bk[boff + 3]])

            for c in range(NC):
                wt, wkey = wnext(wm, c)
                pb = banks[boff + c % 2]
                pk = bk[boff + c % 2]
                fm_proj(pb[:, 0:TT], pk, wt, wkey, lambda dc: src[:, dc, :], srckey)
                while pend:
                    stats_mm(*pend.pop(0))
                i2 = c % 2
                xk = xstk[i2]
                S.dma("sp", xst[i2][:], resid_dram[c * 128:(c + 1) * 128, :], writes=[xk])
                S.op("dve", lambda e, c=c, pb=pb, i2=i2: e.scalar_tensor_tensor(
                    out=Z[:, c, :], in0=xst[i2][:], scalar=ALPHA, in1=pb[:, 0:TT], op0=ALU.mult, op1=ALU.add),
                    reads=[xk, pk], writes=[zk + str(c)])
                S.op("act", lambda e, c=c, i2=i2: e.activation(out=zb[i2][:], in_=Z[:, c, :], func=AF.Copy),
                     reads=[zk + str(c)], writes=[zbk[i2]])
                S.op("act", lambda e, c=c, i2=i2: e.activation(out=zq[i2][:], in_=Z[:, c, :], func=AF.Square),
                     reads=[zk + str(c)], writes=[zqk[i2]])
                pend.append((c, i2))
            while pend:
                stats_mm(*pend.pop(0))
            mean, msq, rstd, nmr = lt
            S.op("act", lambda e: e.activation(out=mean[:], in_=banks[boff + 2][:, 0:TT], func=AF.Copy),
                 reads=[bk[boff + 2]], writes=[kn[0]])
            S.op("act", lambda e: e.activation(out=msq[:], in_=banks[boff + 2][:, 0:TT], func=AF.Square),
                 reads=[bk[boff + 2]], writes=[kn[1]])
            S.op("dve", lambda e: e.tensor_tensor(out=rstd[:], in0=banks[boff + 3][:, 0:TT], in1=msq[:], op=ALU.subtract),
                 reads=[bk[boff + 3], kn[1]], writes=[kn[2]])
            S.op("dve", lambda e: e.tensor_scalar(out=rstd[:], in0=rstd[:], scalar1=LN_EPS, scalar2=None,
                                                   op0=ALU.add), reads=[kn[2]], writes=[kn[2]])
            S.op("act", lambda e: e.activation(out=rstd[:], in_=rstd[:], func=AF.Ln), reads=[kn[2]], writes=[kn[2]])
            S.op("act", lambda e: e.activation(out=rstd[:], in_=rstd[:], func=AF.Exp, scale=-0.5),
                 reads=[kn[2]], writes=[kn[2]])
            S.op("dve", lambda e: e.scalar_tensor_tensor(out=nmr[:], in0=mean[:], scalar=-1.0, in1=rstd[:],
                                                          op0=ALU.mult, op1=ALU.mult),
                 reads=[kn[0], kn[2]], writes=[kn[3]])
            for c in range(NC):
                i2 = c % 2
                ok = ostk[i2]
                S.op("dve", lambda e, c=c: e.tensor_tensor(out=Z[:, c, :], in0=Z[:, c, :], in1=rstd[:], op=ALU.mult),
                     reads=[zk + str(c), kn[2]], writes=[zk + str(c)])
                S.op("dve", lambda e, c=c: e.tensor_tensor(out=Z[:, c, :], in0=Z[:, c, :], in1=nmr[:], op=ALU.add),
                     reads=[zk + str(c), kn[3]], writes=[zk + str(c)])
                S.op("act", lambda e, c=c, i2=i2: e.activation(out=ost[i2][:], in_=Z[:, c, :], func=AF.Identity,
                                                                scale=vec[:, vg, c:c + 1], bias=vec[:, vb, c:c + 1]),
                     reads=[zk + str(c), "vec"], writes=[ok])
                S.dma("sp", dst_dram[c * 128:(c + 1) * 128, :], ost[i2][:], reads=[ok], writes=["dst%s%d_%d" % (tagp, c, S.ninst)])

        with ExitStack() as sa:
            mu = sb("mu", [128, 6, NC], F32, sa)
            lwt = [sb("lwt%d" % i, [LORA, 2, 128], BF16, sa) for i in range(2)]
            rmask = sb("rmask", [128, TT], BF16, sa)
            S.dma("sp", mu[:].rearrange("p a b -> p (a b)"), mud, writes=["mu"])
            S.dma("pool", rmask[:], rmaskd, writes=["rmask"])
            xs = [sb("xs%d" % i, [128, NC, TT], BF16, sa) for i in range(4)]
            ZA = sb("AZ", [128, NC, TT], F32, sa)
            zbf = ZA[:].rearrange("p a b -> p (a b)").bitcast(BF16)
            xs.append(zbf[:, 0:NC * TT].rearrange("p (a b) -> p a b", b=TT))
            xs.append(zbf[:, NC * TT:2 * NC * TT].rearrange("p (a b) -> p a b", b=TT))
            xs = [x if i >= 4 else x[:] for i, x in enumerate(xs)]
            YG = sb("YG", [128, NC, TT], BF16, sa)
            xstage = [sb("xstage%d" % i, [128, TT + 1], F32, sa) for i in range(1)] * 2
            hw_b = sb("hw_b", [LORA, TT], BF16, sa)
            ha_b = sb("ha_b", [LORA, TT], BF16, sa)
            f = [sb("f%d" % i, [128, TT], F32, sa) for i in range(10)]
            b = [sb("b%d" % i, [128, TT], BF16, sa) for i in range(8)]
            fk = ["f%d" % i for i in range(10)]
            bkk = ["b%d" % i for i in range(8)]
            TMs = [[sb("TM%s%d" % (nm, i), [128, NCH, 64], BF16, sa) for nm in "bkvp"] for i in range(2)]
            b4s = [b[4], sb("b4x", [128, TT], BF16, sa)]
            b4k = [bkk[4], "b4x"]
            E1s = [sb("e3x%d" % i, [128, TT], F32, sa) for i in range(2)]
            E1k = ["e3x0", "e3x1"]
            E2s = [sb("e4x%d" % i, [128, TT], F32, sa) for i in range(2)]
            E2k = ["e4x0", "e4x1"]
            gnt = [sb("gnt%d" % i, [128, TT], F32, sa) for i in range(2)]
            xxs = [gnt[0]] * 2
            pLt = [sb("pLt%d" % i, [128, NCH], F32, sa) for i in range(2)]
            AM01 = sb("AM01", [128, 2, NCH, 64], BF16, sa)
            AM23 = sb("AM23", [128, 2, NCH, 64], BF16, sa)
            AM4 = sb("AM4", [128, NCH, 64], BF16, sa)
            XAB = [sb("XAB%d" % i, [128, 2, NCH, 64], BF16, sa) for i in range(2)]
            Tm = sb("Tm", [128, NCH, 64], BF16, sa)
            Sf = sb("Sf", [128, NC, 64], F32, sa)
            Sb_ = sb("Sb", [128, NC, 64], BF16, sa)
            lnbufs = dict(Z=ZA, xst=[(f[4], fk[4]), (f[5], fk[5])], zb=[(b[0], bkk[0]), (b[1], bkk[1])],
                          zq=[(b[2], bkk[2]), (b[3], bkk[3])], lt=[(f[i], fk[i]) for i in range(4)],
                          ost=[(f[6], fk[6]), (f[7], fk[7])])
            zkeys = ["AZ%d" % c for c in range(NC)]

            for s in range(NSEQ):
                S.op("dve", lambda e: e.memset(Sf[:], 0.0), writes=["Sf"])
                S.op("dve", lambda e: e.memset(Sb_[:], 0.0), writes=["Sb"])
                for j in range(NT):
                    t0 = j * TT
                    S.link(zkeys, ["xs4", "xs5"])
                    for dc in range(NC):
                        i2 = dc % 2
                        xk = "xstage0"
                        if j == 0:
                            S.op("dve", lambda e, i2=i2: e.memset(xstage[i2][:, 0:1], 0.0), writes=[xk])
                            S.dma("sp", xstage[i2][:, 1:TT + 1], xT[s, dc * 128:(dc + 1) * 128, 0:TT], writes=[xk])
                        else:
                            S.dma("sp", xstage[i2][:], xT[s, dc * 128:(dc + 1) * 128, t0 - 1:t0 + TT], writes=[xk])
                        S.op("dve", lambda e, i2=i2: e.tensor_tensor(out=xxs[i2][:], in0=xstage[i2][:, 0:TT],
                                                                      in1=xstage[i2][:, 1:TT + 1], op=ALU.subtract),
                             reads=[xk], writes=["gnt0"])
                        for i in range(6):
                            S.op("dve", lambda e, i=i, i2=i2, dc=dc: e.scalar_tensor_tensor(
                                out=xs[i][:, dc, :], in0=xxs[i2][:], scalar=mu[:, i, dc:dc + 1],
                                in1=xstage[i2][:, 1:TT + 1], op0=ALU.mult, op1=ALU.add),
                                reads=["gnt0", xk, "mu"], writes=["xs%d" % i])
                    ck(2)
                    for (wnm, xi, hb, hk, fn) in (("w1", 4, hw_b, "hw_b", AF.Tanh), ("a1", 5, ha_b, "ha_b", AF.Copy)):
                        wt_, wk_ = wnext(wnm, 0)
                        wl = wt_[:].rearrange("p a b -> p (a b)")[:, 0:NC * LORA].rearrange("p (a b) -> p a b", b=LORA)
                        for dc in range(NC):
                            S.op("pe", lambda e, dc=dc, wl=wl, xi=xi: e.matmul(
                                banks[4][0:LORA, 0:TT], lhsT=wl[:, dc, :], rhs=xs[xi][:, dc, :],
                                start=(dc == 0), stop=(dc == NC - 1)),
                                reads=[wk_, "xs%d" % xi], writes=[bk[4]])
                        S.op("act", lambda e, hb=hb, fn=fn: e.activation(out=hb[:], in_=banks[4][0:LORA, 0:TT], func=fn),
                             reads=[bk[4]], writes=[hk])
                    ck(3)
                    def front(c, par, s=s, j=j, t0=t0):
                        pr, pk_, pv, pg = (banks[i][:, 0:TT] for i in range(4))
                        pw, pa = pr, pk_
                        kpw, kpa = bk[0], bk[1]
                        TMb, TMk, TMv, TMp = TMs[par]
                        kTM = ["TM%s%d" % (nm, par) for nm in "bkvp"]
                        lw_ = lwt[par]
                        S.dma("pool", lw_[:].rearrange("p a b -> p (a b)"), lw2d[c], writes=["lwt%d" % par])
                        S.op("pe", lambda e: e.matmul(pw, lhsT=lw_[:, 0, :], rhs=hw_b[:], start=True, stop=True),
                             reads=["lwt%d" % par, "hw_b"], writes=[kpw])
                        S.op("pe", lambda e: e.matmul(pa, lhsT=lw_[:, 1, :], rhs=ha_b[:], start=True, stop=True),
                             reads=["lwt%d" % par, "ha_b"], writes=[kpa])

                        def vc(i, c=c):
                            return vec[:, i, c:c + 1]
                        nw0 = negv[:, 0, c:c + 1]
                        na0 = negv[:, 1, c:c + 1]
                        S.op("act", lambda e: e.activation(out=f[0][:], in_=pw, func=AF.Exp, scale=-1.0, bias=nw0),
                             reads=[kpw, "negv"], writes=[fk[0]])
                        S.op("act", lambda e: e.activation(out=f[0][:], in_=f[0][:], func=AF.Ln, bias=1.0),
                             reads=[fk[0]], writes=[fk[0]])
                        S.op("act", lambda e: e.activation(out=f[0][:], in_=f[0][:], func=AF.Exp, scale=-1.0),
                             reads=[fk[0]], writes=[fk[0]])
                        S.op("act", lambda e: e.activation(out=f[4][:], in_=pa, func=AF.Exp, scale=-1.0, bias=na0),
                             reads=[kpa, "negv"], writes=[fk[4]])
                        S.op("act", lambda e: e.activation(out=f[4][:], in_=f[4][:], func=AF.Ln, bias=1.0),
                             reads=[fk[4]], writes=[fk[4]])
                        S.op("act", lambda e: e.activation(out=f[4][:], in_=f[4][:], func=AF.Exp, scale=-1.0),
                             reads=[fk[4]], writes=[fk[4]])
                        ck(41)
                        for m, xi, pbank, pkey in ((W_V, 2, pv, bk[2]), (W_G, 3, pg, bk[3]),
                                                   (W_R, 0, pr, bk[0]), (W_K, 1, pk_, bk[1])):
                            wt, wkey = wnext(m, c)
                            fm_proj(pbank, pkey, wt, wkey, lambda dc, xi=xi: xs[xi][:, dc, :], "xs%d" % xi)
                            ck(420 + xi)
                        S.op("act", lambda e: e.activation(out=f[9][:], in_=pg, func=AF.Exp, scale=-1.0),
                             reads=[bk[3]], writes=[fk[9]])
                        S.op("act", lambda e: e.activation(out=f[9][:], in_=f[9][:], func=AF.Ln, bias=1.0),
                             reads=[fk[9]], writes=[fk[9]])
                        S.op("act", lambda e: e.activation(out=f[9][:], in_=f[9][:], func=AF.Exp, scale=-1.0),
                             reads=[fk[9]], writes=[fk[9]])
                        S.op("dve", lambda e: e.tensor_tensor(out=f[9][:], in0=pg, in1=f[9][:], op=ALU.mult),
                             reads=[bk[3], fk[9]], writes=[fk[9]])
                        S.op("act", lambda e: e.activation(out=b[5][:], in_=pv, func=AF.Copy),
                             reads=[bk[2]], writes=[bkk[5]])
                        pss = banks[3][:, 0:TT]
                        pbs = banks[2][:, 0:TT]
                        ck(43)
                        S.op("dve", lambda e: e.tensor_tensor_scan(out=f[1][:], data0=rmask[:], data1=f[0][:],
                                                                    initial=0.0, op0=ALU.mult, op1=ALU.add),
                             reads=["rmask", fk[0]], writes=[fk[1]])
                        ck(501)
                        S.op("act", lambda e: e.activation(out=f[2][:], in_=f[1][:], func=AF.Exp, scale=-C0),
                             reads=[fk[1]], writes=[fk[2]])
                        ck(502)
                        S.op("act", lambda e: e.activation(out=f[3][:], in_=f[1][:], func=AF.Exp, scale=C0),
                             reads=[fk[1]], writes=[fk[3]])
                        ck(503)
                        S.op("dve", lambda e: e.tensor_tensor(out=f[0][:], in0=f[1][:], in1=f[0][:], op=ALU.subtract),
                             reads=[fk[1], fk[0]], writes=[fk[0]])
                        ck(504)
                        S.op("act", lambda e: e.activation(out=f[0][:], in_=f[0][:], func=AF.Exp, scale=-C0),
                             reads=[fk[0]], writes=[fk[0]])
                        ck(505)
                        S.op("dve", lambda e: e.tensor_scalar(out=f[5][:], in0=pk_, scalar1=vc(V_KK), scalar2=None,
                                                               op0=ALU.mult), reads=[bk[1], "vec"], writes=[fk[5]])
                        ck(506)
                        S.op("act", lambda e: e.activation(out=b[0][:], in_=pk_, func=AF.Square, scale=vc(V_KK)),
                             reads=[bk[1], "vec", fk[5]], writes=[bkk[0]])
                        ck(507)
                        S.op("pe", lambda e: e.matmul(pss, lhsT=bones[:], rhs=b[0][:], start=True, stop=True),
                             reads=["bones", bkk[0]], writes=[bk[3]])
                        ck(508)
                        S.op("dve", lambda e: e.tensor_scalar(out=f[6][:], in0=pss, scalar1=1e-19, scalar2=None, op0=ALU.max),
                             reads=[bk[3]], writes=[fk[6]])
                        S.op("act", lambda e: e.activation(out=f[6][:], in_=f[6][:], func=AF.Ln), reads=[fk[6]], writes=[fk[6]])
                        S.op("act", lambda e: e.activation(out=f[6][:], in_=f[6][:], func=AF.Exp, scale=-0.5),
                             reads=[fk[6]], writes=[fk[6]])
                        ck(509)
                        ck(511)
                        S.op("dve", lambda e: e.tensor_tensor(out=f[5][:], in0=f[5][:], in1=f[6][:], op=ALU.mult),
                             reads=[fk[5], fk[6]], writes=[fk[5]])
                        ck(512)
                        S.op("dve", lambda e: e.tensor_tensor(out=f[7][:], in0=f[5][:], in1=f[4][:], op=ALU.mult),
                             reads=[fk[5], fk[4]], writes=[fk[7]])
                        ck(513)
                        S.op("dve", lambda e, c=c: e.tensor_scalar(out=f[4][:], in0=f[4][:], scalar1=vc(V_KA),
                                                                    scalar2=omka[:, c:c + 1], op0=ALU.mult, op1=ALU.add),
                             reads=[fk[4], "vec", "omka"], writes=[fk[4]])
                        ck(514)
                        S.op("dve", lambda e: e.tensor_tensor(out=f[8][:], in0=pk_, in1=f[4][:], op=ALU.mult),
                             reads=[bk[1], fk[4]], writes=[fk[8]])
                        ck(515)
                        S.op("dve", lambda e: e.tensor_tensor(out=b[1][:], in0=f[5][:], in1=f[0][:], op=ALU.mult),
                             reads=[fk[5], fk[0]], writes=[bkk[1]])
                        ck(516)
                        S.op("dve", lambda e: e.tensor_tensor(out=b[2][:], in0=f[7][:], in1=f[3][:], op=ALU.mult),
                             reads=[fk[7], fk[3]], writes=[bkk[2]])
                        ck(517)
                        S.op("dve", lambda e: e.tensor_tensor(out=b[3][:], in0=f[8][:], in1=f[3][:], op=ALU.mult),
                             reads=[fk[8], fk[3]], writes=[bkk[3]])
                        ck(518)
                        S.op("dve", lambda e: e.tensor_tensor(out=b4s[par][:], in0=pr, in1=f[2][:], op=ALU.mult),
                             reads=[bk[0], fk[2]], writes=[b4k[par]])
                        ck(519)
                        S.op("act", lambda e: e.activation(
                            out=pLt[par][:], in_=f[2][:].rearrange("p (a b) -> p a b", b=L)[:, :, L - 1], func=AF.Copy),
                            reads=[fk[2]], writes=["pLt%d" % par])
                        ck(520)
                        S.op("dve", lambda e: e.tensor_tensor(out=f[6][:], in0=pr, in1=f[8][:], op=ALU.mult),
                             reads=[bk[0], fk[8]], writes=[fk[6]])
                        ck(521)
                        S.op("act", lambda e: e.activation(out=b[0][:], in_=f[6][:], func=AF.Copy, scale=vc(V_RK)),
                             reads=[fk[6], "vec"], writes=[bkk[0]])
                        ck(522)
                        S.op("pe", lambda e: e.matmul(pbs, lhsT=bones[:], rhs=b[0][:], start=True, stop=True),
                             reads=["bones", bkk[0]], writes=[bk[2]])
                        ck(523)
                        S.op("dve", lambda e: e.tensor_tensor(out=f[5][:], in0=pbs, in1=b[5][:], op=ALU.mult),
                             reads=[bk[2], bkk[5]], writes=[fk[5]])
                        ck(524)
                        S.op("dve", lambda e: e.tensor_scalar(out=E1s[par][:], in0=f[9][:], scalar1=vc(V_GNG), scalar2=None,
                                                               op0=ALU.mult), reads=[fk[9], "vec"], writes=[E1k[par]])
                        ck(525)
                        S.op("dve", lambda e: e.scalar_tensor_tensor(out=E2s[par][:], in0=f[5][:], scalar=vc(V_GNB),
                                                                      in1=f[9][:], op0=ALU.add, op1=ALU.mult),
                             reads=[fk[5], fk[9], "vec"], writes=[E2k[par]])
                        ck(526)
                        ck(4)
                        for ti, (src, skey, dstt, dkey) in enumerate(((b[2], bkk[2], TMb, kTM[0]), (b[3], bkk[3], TMk, kTM[1]),
                                                                      (b[5], bkk[5], TMv, kTM[2]), (b[1], bkk[1], TMp, kTM[3]))):
                            pbk = ti % 2
                            ptile = banks[pbk][:].bitcast(BF16)[:, 0:NCH * 64].rearrange("p (a b) -> p a b", b=64)
                            for cc in range(NCH):
                                for hh in range(2):
                                    hs = slice(hh * 64, hh * 64 + 64)
                                    S.op("pe", lambda e, src=src, cc=cc, ptile=ptile, hs=hs: e.transpose(
                                        ptile[hs, cc, :], src[hs, cc * L:(cc + 1) * L], ident[hs, hs]),
                                        reads=[skey, "ident"], writes=[bk[pbk]])
                            S.op("act", lambda e, dstt=dstt, ptile=ptile: e.activation(out=dstt[:], in_=ptile, func=AF.Copy),
                                 reads=[bk[pbk]], writes=[dkey])
                        ck(5)

                    def scan(c, par, s=s, j=j, t0=t0):
                        TMb, TMk, TMv, TMp = TMs[par]
                        kTM = ["TM%s%d" % (nm, par) for nm in "bkvp"]
                        r4, r4k = b4s[par], b4k[par]
                        GS = NCH // 2
                        grp = [slice(0, GS), slice(GS, NCH)]
                        W4 = GS * 64

                        def pv2(bank):
                            return banks[bank][:, 0:2 * W4].rearrange("p (t a b) -> p t a b", t=2, b=64)

                        def pv1(bank):
                            return banks[bank][:, 0:W4].rearrange("p (a b) -> p a b", b=64)

                        def hsl(hh):
                            return slice(hh * 64, hh * 64 + 64)

                        am_banks = [(4, 5, 6), (4, 5, 6)]
                        if S.inter is not None:
                            lst_ = S.inter[0]
                            nsafe = 0
                            for it_ in lst_:
                                if any(w in (bkk[1], bkk[2], bkk[3]) for w in it_[-1]):
                                    break
                                nsafe += 1
                            S.guard = len(lst_) - nsafe
                        for g in range(2):
                            bA, bB, bC = am_banks[g]
                            specs = ((bA, 0, b[2], b[1], bkk[2], bkk[1]),
                                     (bA, 1, b[1], b[2], bkk[1], bkk[2]),
                                     (bB, 0, b[1], b[3], bkk[1], bkk[3]),
                                     (bB, 1, b[2], r4, bkk[2], r4k),
                                     (bC, 0, b[3], r4, bkk[3], r4k))
                            for (bnk, half, lt_, rt_, lk, rk_) in specs:
                                pv_ = pv2(bnk)
                                for ci, cc in enumerate(range(g * GS, (g + 1) * GS)):
                                    cs = slice(cc * L, (cc + 1) * L)
                                    for hh in range(2):
                                        hs = hsl(hh)
                                        S.op("pe", lambda e, pv_=pv_, half=half, ci=ci, lt_=lt_, rt_=rt_, hs=hs, cs=cs: e.matmul(
                                            pv_[hs, half, ci, :], lhsT=lt_[hs, cs], rhs=rt_[hs, cs], start=True, stop=True),
                                            reads=[lk, rk_], writes=[bk[bnk]])
                            S.op("dve", lambda e, bA=bA, g=g: e.tensor_tensor(
                                out=AM01[:, :, grp[g], :], in0=pv2(bA), in1=m3[:, 0:2, 0:GS, :], op=ALU.mult),
                                reads=[bk[bA], "m3"], writes=["AM01" + str(g)])
                            S.op("dve", lambda e, bB=bB, g=g: e.tensor_tensor(
                                out=AM23[:, :, grp[g], :], in0=pv2(bB), in1=m3[:, 1:3, 0:GS, :], op=ALU.mult),
                                reads=[bk[bB], "m3"], writes=["AM23" + str(g)])
                            S.op("dve", lambda e, bC=bC, g=g: e.tensor_tensor(
                                out=AM4[:, grp[g], :], in0=pv1(bC), in1=m3[:, 2, 0:GS, :], op=ALU.mult),
                                reads=[bk[bC], "m3"], writes=["AM4" + str(g)])
                            S.op("dve", lambda e, g=g: e.tensor_tensor(out=Tm[:, grp[g], :], in0=idm4[:, 0:GS, :],
                                                                        in1=AM01[:, 0, grp[g], :], op=ALU.subtract),
                                 reads=["idm4", "AM01" + str(g)], writes=["Tm" + str(g)])
                        ck(6)
                        S.guard = None
                        Xc, Xck = AM01, "AM01"
                        for lev in range(1, 6):
                            Xn = XAB[lev % 2]
                            Xnk = "XAB%d" % (lev % 2)
                            for g in range(2):
                                bkx = 4 + g
                                px = pv2(bkx)
                                for ci, cc in enumerate(range(g * GS, (g + 1) * GS)):
                                    for hh in range(2):
                                        hs = hsl(hh)
                                        if lev < 5:
                                            S.op("pe", lambda e, px=px, ci=ci, cc=cc, hs=hs, Xc=Xc: e.matmul(
                                                px[hs, 0, ci, :], lhsT=Xc[hs, 1, cc, :], rhs=Xc[hs, 0, cc, :], start=True, stop=True),
                                                reads=[Xck + str(g)], writes=[bk[bkx]])
                                        S.op("pe", lambda e, px=px, ci=ci, cc=cc, hs=hs, Xc=Xc: e.matmul(
                                            px[hs, 1, ci, :], lhsT=Xc[hs, 0, cc, :], rhs=Xc[hs, 1, cc, :], start=True, stop=True),
                                            reads=[Xck + str(g)], writes=[bk[bkx]])
                                if lev < 5:
                                    S.op("act", lambda e, px=px, Xn=Xn, g=g: e.activation(
                                        out=Xn[:, :, grp[g], :], in_=px, func=AF.Copy),
                                        reads=[bk[bkx]], writes=[Xnk + str(g)])
                                else:
                                    S.op("act", lambda e, px=px, Xn=Xn, g=g: e.activation(
                                        out=Xn[:, 1, grp[g], :], in_=px[:, 1], func=AF.Copy),
                                        reads=[bk[bkx]], writes=[Xnk + str(g)])
                            for g in range(2):
                                bkt = 6 + g
                                ptp = pv1(bkt)
                                for ci, cc in enumerate(range(g * GS, (g + 1) * GS)):
                                    for hh in range(2):
                                        hs = hsl(hh)
                                        S.op("pe", lambda e, ptp=ptp, ci=ci, cc=cc, hs=hs, Xn=Xn: e.matmul(
                                            ptp[hs, ci, :], lhsT=Xn[hs, 1, cc, :], rhs=Tm[hs, cc, :], start=True, stop=True),
                                            reads=[Xnk + str(g), "Tm" + str(g)], writes=[bk[bkt]])
                                S.op("dve", lambda e, ptp=ptp, g=g: e.tensor_tensor(out=Tm[:, grp[g], :], in0=ptp,
                                                                                    in1=Tm[:, grp[g], :], op=ALU.add),
                                     reads=[bk[bkt], "Tm" + str(g)], writes=["Tm" + str(g)])
                            Xc, Xck = Xn, Xnk
                        ck(7)
                        for g in range(2):
                            bkx = 4 + g
                            pgv = pv2(bkx)
                            for ci, cc in enumerate(range(g * GS, (g + 1) * GS)):
                                for hh in range(2):
                                    hs = hsl(hh)
                                    S.op("pe", lambda e, pgv=pgv, ci=ci, cc=cc, hs=hs: e.matmul(
                                        pgv[hs, 0, ci, :], lhsT=Tm[hs, cc, :], rhs=TMp[hs, cc, :], start=True, stop=True),
                                        reads=["Tm" + str(g), kTM[3]], writes=[bk[bkx]])
                                    S.op("pe", lambda e, pgv=pgv, ci=ci, cc=cc, hs=hs: e.matmul(
                                        pgv[hs, 1, ci, :], lhsT=Tm[hs, cc, :], rhs=AM23[hs, 0, cc, :], start=True, stop=True),
                                        reads=["Tm" + str(g), "AM23" + str(g)], writes=[bk[bkx]])
                            S.op("act", lambda e, pgv=pgv, g=g: e.activation(out=AM01[:, :, grp[g], :], in_=pgv, func=AF.Copy),
                                 reads=[bk[bkx]], writes=["AM01" + str(g)])
                        for g in range(2):
                            b0_, b1_ = 4 + g, 6 + g
                            pDN, pRA = pv2(b0_), pv2(b1_)
                            k0, k1 = bk[b0_], bk[b1_]
                            for ci, cc in enumerate(range(g * GS, (g + 1) * GS)):
                                for hh in range(2):
                                    hs = hsl(hh)
                                    for li in range(2):
                                        S.op("pe", lambda e, pDN=pDN, ci=ci, cc=cc, hs=hs, li=li: e.matmul(
                                            pDN[hs, li, ci, :], lhsT=AM01[hs, li, cc, :], rhs=TMb[hs, cc, :], start=True, stop=True),
                                            reads=["AM01" + str(g), kTM[0]], writes=[k0])
                                    for li in range(2):
                                        S.op("pe", lambda e, pRA=pRA, ci=ci, cc=cc, hs=hs, li=li: e.matmul(
                                            pRA[hs, li, ci, :], lhsT=AM01[hs, li, cc, :], rhs=AM23[hs, 1, cc, :], start=True, stop=True),
                                            reads=["AM01" + str(g), "AM23" + str(g)], writes=[k1])
                            S.op("act", lambda e, pDN=pDN, g=g: e.activation(out=XAB[0][:, 0, grp[g], :], in_=pDN[:, 0],
                                                                               func=AF.Copy, scale=-1.0),
                                 reads=[k0], writes=["XAB0" + str(g)])
                            S.op("dve", lambda e, pDN=pDN, g=g: e.tensor_tensor(out=XAB[0][:, 1, grp[g], :], in0=TMk[:, grp[g], :],
                                                                                 in1=pDN[:, 1], op=ALU.subtract),
                                 reads=[k0, kTM[1]], writes=["XAB0" + str(g)])
                            S.op("dve", lambda e, pRA=pRA, g=g: e.tensor_tensor(
                                out=XAB[1][:, 0, grp[g], :],
                                in0=r4[:, g * W4:(g + 1) * W4].rearrange("p (a b) -> p a b", b=64), in1=pRA[:, 0], op=ALU.subtract),
                                reads=[k1, r4k], writes=["XAB1" + str(g)])
                            S.op("dve", lambda e, pRA=pRA, g=g: e.tensor_tensor(out=XAB[1][:, 1, grp[g], :], in0=AM4[:, grp[g], :],
                                                                                 in1=pRA[:, 1], op=ALU.subtract),
                                 reads=[k1, "AM4" + str(g)], writes=["XAB1" + str(g)])
                        py = banks[7]
                        for cc in range(NCH):
                            g = cc // GS
                            cs = slice(cc * L, (cc + 1) * L)
                            pS = banks[4 + (cc % 2)][:, 0:64]
                            pSk = bk[4 + (cc % 2)]
                            pl = pLt[par][:, cc:cc + 1]
                            plk = "pLt%d" % par
                            S.op("dve", lambda e, c=c, pl=pl: e.tensor_scalar(out=Sf[:, c, :], in0=Sf[:, c, :], scalar1=pl,
                                                                               scalar2=None, op0=ALU.mult),
                                 reads=["Sf", plk], writes=["Sf"])
                            for hh in range(2):
                                hs = hsl(hh)
                                S.op("pe", lambda e, hs=hs, cc=cc, c=c, pS=pS: e.matmul(
                                    pS[hs, :], lhsT=XAB[0][hs, 0, cc, :], rhs=Sb_[hs, c, :], start=True, stop=False),
                                    reads=["XAB0" + str(g), "Sb"], writes=[pSk])
                                S.op("pe", lambda e, hs=hs, cc=cc, pS=pS: e.matmul(
                                    pS[hs, :], lhsT=XAB[0][hs, 1, cc, :], rhs=TMv[hs, cc, :], start=False, stop=True),
                                    reads=["XAB0" + str(g), kTM[2]], writes=[pSk])
                            for hh in range(2):
                                hs = hsl(hh)
                                S.op("pe", lambda e, hs=hs, cs=cs, cc=cc, c=c: e.matmul(
                                    py[hs, cs], lhsT=Sb_[hs, c, :], rhs=XAB[1][hs, 0, cc, :], start=True, stop=False),
                                    reads=["Sb", "XAB1" + str(g)], writes=[bk[7]])
                                S.op("pe", lambda e, hs=hs, cs=cs, cc=cc: e.matmul(
                                    py[hs, cs], lhsT=TMv[hs, cc, :], rhs=XAB[1][hs, 1, cc, :], start=False, stop=True),
                                    reads=[kTM[2], "XAB1" + str(g)], writes=[bk[7]])
                            S.op("dve", lambda e, c=c, pl=pl, pS=pS: e.scalar_tensor_tensor(
                                out=Sb_[:, c, :], in0=pS, scalar=pl, in1=Sf[:, c, :], op0=ALU.mult, op1=ALU.add),
                                reads=[pSk, "Sf", plk], writes=["Sb"])
                            S.op("dve", lambda e, c=c, pl=pl, pS=pS: e.scalar_tensor_tensor(
                                out=Sf[:, c, :], in0=pS, scalar=pl, in1=Sf[:, c, :], op0=ALU.mult, op1=ALU.add),
                                reads=[pSk, "Sf", plk], writes=["Sf"])
                            ck(8)
                        pyv = banks[7][:, 0:TT]
                        if S.inter is not None:
                            S.pump(S.inter[0], len(S.inter[0]))
                        g0, g1 = gnt
                        E1, E2 = E1s[par], E2s[par]
                        S.op("act", lambda e: e.activation(out=b[6][:], in_=pyv, func=AF.Copy), reads=[bk[7]], writes=[bkk[6]])
                        S.op("act", lambda e: e.activation(out=b[7][:], in_=pyv, func=AF.Square), reads=[bk[7]], writes=[bkk[7]])
                        S.op("pe", lambda e: e.matmul(banks[0][:, 0:TT], lhsT=bones[:], rhs=b[6][:], start=True, stop=True),
                             reads=["bones", bkk[6]], writes=[bk[0]])
                        S.op("pe", lambda e: e.matmul(banks[1][:, 0:TT], lhsT=bones[:], rhs=b[7][:], start=True, stop=True),
                             reads=["bones", bkk[7]], writes=[bk[1]])
                        S.op("act", lambda e: e.activation(out=g0[:], in_=banks[0][:, 0:TT], func=AF.Copy, scale=1.0 / 64),
                             reads=[bk[0]], writes=["gnt0"])
                        S.op("act", lambda e: e.activation(out=g1[:], in_=banks[0][:, 0:TT], func=AF.Square, scale=1.0 / 64),
                             reads=[bk[0]], writes=["gnt1"])
                        S.op("dve", lambda e: e.scalar_tensor_tensor(out=g1[:], in0=banks[1][:, 0:TT], scalar=1.0 / 64,
                                                                      in1=g1[:], op0=ALU.mult, op1=ALU.subtract),
                             reads=[bk[1], "gnt1"], writes=["gnt1"])
                        S.op("dve", lambda e: e.tensor_scalar(out=g1[:], in0=g1[:], scalar1=GN_EPS, scalar2=None,
                                                               op0=ALU.add), reads=["gnt1"], writes=["gnt1"])
                        S.op("act", lambda e: e.activation(out=g1[:], in_=g1[:], func=AF.Ln), reads=["gnt1"], writes=["gnt1"])
                        S.op("act", lambda e: e.activation(out=g1[:], in_=g1[:], func=AF.Exp, scale=-0.5),
                             reads=["gnt1"], writes=["gnt1"])
                        S.op("dve", lambda e: e.scalar_tensor_tensor(out=g0[:], in0=g0[:], scalar=-1.0, in1=g1[:],
                                                                      op0=ALU.mult, op1=ALU.mult),
                             reads=["gnt0", "gnt1"], writes=["gnt0"])
                        S.op("dve", lambda e: e.tensor_tensor(out=g1[:], in0=pyv, in1=g1[:], op=ALU.mult),
                             reads=[bk[7], "gnt1"], writes=["gnt1"])
                        S.op("dve", lambda e: e.tensor_tensor(out=g1[:], in0=g1[:], in1=g0[:], op=ALU.add),
                             reads=["gnt1", "gnt0"], writes=["gnt1"])
                        S.op("dve", lambda e: e.tensor_tensor(out=g1[:], in0=g1[:], in1=E1[:], op=ALU.mult),
                             reads=["gnt1", E1k[par]], writes=["gnt1"])
                        S.op("dve", lambda e, c=c: e.tensor_tensor(out=YG[:, c, :], in0=g1[:], in1=E2[:], op=ALU.add),
                             reads=["gnt1", E2k[par]], writes=["YG"])
                        if debug == 2:
                            S.op("act", lambda e: e.activation(out=g0[:], in_=pyv, func=AF.Copy), reads=[bk[7]], writes=["gnt0"])
                            S.dma("sp", dbg[s][c * 128:(c + 1) * 128, t0:t0 + TT], g0[:], reads=["gnt0"],
                                  writes=["dbgx%d" % S.ninst])
                        ck(9)

                    front(0, 0)
                    for c in range(NC):
                        nxt = deque()
                        if c + 1 < NC:
                            S.defer = nxt
                            front(c + 1, (c + 1) % 2)
                            S.defer = None
                        S.inter = (nxt, FRONT_RATIO)
                        S._acc = 0.0
                        S.npump = 0
                        scan(c, c % 2)
                        S.inter = None
                        S.pump(nxt, len(nxt))
                    if debug == 1:
                        S.dma("pool", dbg[s][:, t0:t0 + TT].rearrange("(c p) t -> p c t", p=128), YG[:],
                              reads=["YG"], writes=["dbg%d" % S.ninst])
                    S.link(["xs4", "xs5"], zkeys)
                    run_outproj_ln(lnbufs, W_AOUT, YG, "YG", xT[s][:, t0:t0 + TT], x1f[s][:, t0:t0 + TT],
                                   V_LNG0, V_LNB0, "A")

        ck(10)
        S.barrier()
        with ExitStack() as sb1:
            X1B = sb("X1B", [128, NC, TT], BF16, sb1)
            ostg = [sb("ostg%d" % i, [128, NC, TT], BF16, sb1) for i in range(2)]
            vstg = sb("vstg", [128, NB, C], BF16, sb1)
            vtmp = [sb("vtmp%d" % i, [128, TT], BF16, sb1) for i in range(2)]
            for s in range(NSEQ):
                for j in range(NT):
                    t0 = j * TT
                    S.dma("pool", X1B[:], x1f[s][:, t0:t0 + TT].rearrange("(c p) t -> p c t", p=128),
                          reads=["dstA"], writes=["X1B"])
                    for mi, m in enumerate((W_BK, W_BV, W_BQ, W_BG)):
                        og = ostg[mi % 2]
                        ogk = "ostg%d" % (mi % 2)
                        for c in range(NC):
                            wt, wkey = wnext(m, c)
                            pb = banks[c % 2][:, 0:TT]
                            pk = bk[c % 2]
                            fm_proj(pb, pk, wt, wkey, lambda dc: X1B[:, dc, :], "X1B")
                            if m == W_BG:
                                S.op("act", lambda e, og=og, c=c, pb=pb: e.activation(out=og[:, c, :], in_=pb, func=AF.Silu),
                                     reads=[pk], writes=[ogk])
                            elif m == W_BV:
                                i2 = c % 2
                                S.op("act", lambda e, i2=i2, pb=pb: e.activation(out=vtmp[i2][:], in_=pb, func=AF.Copy),
                                     reads=[pk], writes=["vtmp%d" % i2])
                                ptile = banks[2 + i2][:].bitcast(BF16)[:, 0:NB * 128].rearrange("p (a b) -> p a b", b=128)
                                for n in range(NB):
                                    S.op("pe", lambda e, i2=i2, n=n, ptile=ptile: e.transpose(
                                        ptile[:, n, :], vtmp[i2][:, n * 128:(n + 1) * 128], ident[:]),
                                        reads=["vtmp%d" % i2, "ident"], writes=[bk[2 + i2]])
                                S.op("dve", lambda e, c=c, ptile=ptile: e.tensor_copy(
                                    out=vstg[:, :, c * 128:(c + 1) * 128], in_=ptile),
                                    reads=[bk[2 + i2]], writes=["vstg"])
                            else:
                                S.op("act", lambda e, og=og, c=c, pb=pb: e.activation(out=og[:, c, :], in_=pb, func=AF.Copy),
                                     reads=[pk], writes=[ogk])
                        if m == W_BV:
                            S.dma("sp", VTd[s, j * NB:(j + 1) * NB].rearrange("n p c -> p n c"), vstg[:],
                                  reads=["vstg"], writes=["VTd%d" % S.ninst])
                        else:
                            dst = {W_BK: KTd, W_BQ: QTd, W_BG: SGd}[m]
                            S.dma("sp", dst[s][:, :, t0:t0 + TT].rearrange("c p t -> p c t"), og[:],
                                  reads=[ogk], writes=["scrB%d" % S.ninst])
        ck(11)
        S.barrier()
        with ExitStack() as sb2:
            lam_t = sb("lam_t", [128, 4, 128], F32, sb2)
            subg = sb("subg", [128, 256], F32, sb2)
            u4 = sb("u4", [128, HB, 512], F32, sb2)
            ud4 = sb("ud4", [128, HB, 512], F32, sb2)
            cbias = sb("cbias", [128, HB * 4], F32, sb2)
            lsc = sb("lsc", [128, 8], F32, sb2)
            ljunk = sb("ljunk", [128, 128], F32, sb2)
            S.dma("sp", lam_t[:].rearrange("p a b -> p (a b)"), lamd, writes=["lam_t"])
            S.dma("sp", subg[:], subgd, writes=["subg"])
            S.dma("sp", u4[:].rearrange("p a b -> p (a b)"), ualid, writes=["u4"])
            S.dma("sp", ud4[:].rearrange("p a b -> p (a b)"), dalid, writes=["ud4"])
            S.dma("sp", cbias[:], slpd, writes=["cbias"])
            for i in range(2):
                S.op("dve", lambda e, i=i: e.tensor_tensor(out=ljunk[:], in0=lam_t[:, 2 * i, :], in1=lam_t[:, 2 * i + 1, :],
                                                            op=ALU.mult), reads=["lam_t"], writes=["ljunk"])
                S.op("dve", lambda e, i=i: e.tensor_reduce(out=lsc[:, i:i + 1], in_=ljunk[:], axis=AX.X, op=ALU.add),
                     reads=["ljunk"], writes=["lsc"])
            S.op("act", lambda e: e.activation(out=lsc[:, 2:4], in_=lsc[:, 0:2], func=AF.Exp), reads=["lsc"], writes=["lsc"])
            S.op("dve", lambda e: e.tensor_tensor(out=lsc[:, 4:5], in0=lsc[:, 3:4], in1=lsc[:, 2:3], op=ALU.subtract),
                 reads=["lsc"], writes=["lsc"])
            S.op("dve", lambda e: e.tensor_scalar(out=lsc[:, 4:5], in0=lsc[:, 4:5], scalar1=-LAM_INIT, scalar2=None,
                                                   op0=ALU.add), reads=["lsc"], writes=["lsc"])
            S.op("dve", lambda e: e.tensor_scalar(out=subg[:], in0=subg[:], scalar1=1.0 - LAM_INIT, scalar2=None,
                                                   op0=ALU.mult), reads=["subg"], writes=["subg"])
            KTs = [sb("KT%d" % i, [128, 2, T], BF16, sb2) for i in range(2)]
            QTs = [sb("QT%d" % i, [128, 2, T], BF16, sb2) for i in range(2)]
            SGs = [sb("SG%d" % i, [128, 2, T], BF16, sb2) for i in range(2)]
            OGs = [sb("OG%d" % i, [128, 2, T], BF16, sb2) for i in range(2)]
            VTs = [sb("VT%d" % i, [128, NQB, 258], BF16, sb2) for i in range(2)]
            for i in range(2):
                S.op("dve", lambda e, i=i: e.memset(VTs[i][:, :, 256:258], 1.0), writes=["VT%d" % i])
            NR = 3
            stmp = [sb("stmp%d" % i, [128, 512], F32, sb2) for i in range(NR)]
            ptb = [sb("ptb%d" % i, [128, 512], BF16, sb2) for i in range(NR)]
            osb = [sb("osb%d" % i, [128, 256], F32, sb2) for i in range(2)]
            ojk = sb("ojk", [128, 256], F32, sb2)
            onb = [sb("onb%d" % i, [128, 256], BF16, sb2) for i in range(2)]
            rs = [sb("rs%d" % i, [128, 8], F32, sb2) for i in range(2)]
            scale = 128 ** -0.5
            LAG = 2

            def attn_head(s, h, ib):
                KT, QT, SG, OG, VT = KTs[ib], QTs[ib], SGs[ib], OGs[ib], VTs[ib]
                kK, kQ, kS, kO, kV = ("KT%d" % ib, "QT%d" % ib, "SG%d" % ib, "OG%d" % ib, "VT%d" % ib)
                S.dma("sp", KT[:], KTd[s, 2 * h:2 * h + 2].rearrange("c p t -> p c t"), writes=[kK])
                S.dma("sp", QT[:], QTd[s, 2 * h:2 * h + 2].rearrange("c p t -> p c t"), writes=[kQ])
                S.dma("sp", VT[:, :, 0:256], VTd[s][:, :, h * 256:(h + 1) * 256].rearrange("n p c -> p n c"), writes=[kV])
                S.dma("sp", SG[:], SGd[s, 2 * h:2 * h + 2].rearrange("c p t -> p c t"), writes=[kS])
                slope = 2.0 ** (-(8.0 / HB) * (h + 1))
                groups = []
                for qb in range(NQB):
                    for m in range(2):
                        ng = (qb + 4) // 4
                        for g in range(ng - 1, -1, -1):
                            hi = qb - 4 * g
                            lo = max(0, hi - 3)
                            groups.append((qb, m, g, lo, hi, g == ng - 1, g == 0))
                n = len(groups)

                def scores(i):
                    qb, m, g, lo, hi, first, last = groups[i]
                    r = i % NR
                    qs = slice(qb * 128, (qb + 1) * 128)
                    j0 = 3 - (hi - lo)
                    pst = banks[r]
                    for kb in range(lo, hi + 1):
                        j = j0 + (kb - lo)
                        S.op("pe", lambda e, pst=pst, m=m, kb=kb, qs=qs, j=j: e.matmul(
                            pst[:, j * 128:(j + 1) * 128], lhsT=KT[:, m, kb * 128:(kb + 1) * 128], rhs=QT[:, m, qs],
                            start=True, stop=True), reads=[kK, kQ], writes=[bk[r]])
                    bias_t = ud4 if g == 0 else u4
                    cs_ = slice(j0 * 128, 512)
                    S.op("dve", lambda e, pst=pst, r=r, bias_t=bias_t, cs_=cs_: e.scalar_tensor_tensor(
                        out=stmp[r][:, cs_], in0=pst[:, cs_], scalar=scale, in1=bias_t[:, h, cs_], op0=ALU.mult, op1=ALU.add),
                        reads=[bk[r], "u4", "ud4"], writes=["stmp%d" % r])
                    cst = -slope * 512.0 * g
                    S.op("act", lambda e, r=r, cst=cst, cs_=cs_: e.activation(
                        out=ptb[r][:, cs_], in_=stmp[r][:, cs_], func=AF.Exp, bias=cbias[:, h * 4 + g:h * 4 + g + 1]),
                        reads=["stmp%d" % r, "cbias"], writes=["ptb%d" % r])

                def pv(i):
                    qb, m, g, lo, hi, first, last = groups[i]
                    r = i % NR
                    par = qb % 2
                    j0 = 3 - (hi - lo)
                    pob = banks[4 + 2 * par + m]
                    pok = bk[4 + 2 * par + m]
                    for kb in range(lo, hi + 1):
                        j = j0 + (kb - lo)
                        S.op("pe", lambda e, pob=pob, r=r, kb=kb, j=j, st_=(first and kb == lo), sp_=(last and kb == hi):
                             e.matmul(pob[:, 0:257], lhsT=ptb[r][:, j * 128:(j + 1) * 128], rhs=VT[:, kb, 0:257],
                                      start=st_, stop=sp_), reads=["ptb%d" % r, kV], writes=[pok])
                    if last and m == 1:
                        S.pump(ep_q, len(ep_q))
                        S.defer = ep_q
                        epilogue(qb)
                        S.defer = None

                def epilogue(qb):
                    par = qb % 2
                    qs = slice(qb * 128, (qb + 1) * 128)
                    p0, p1 = banks[4 + 2 * par], banks[5 + 2 * par]
                    k0, k1 = bk[4 + 2 * par], bk[5 + 2 * par]
                    rs_, rk = rs[par], "rs%d" % par
                    ob, obk = osb[par], "osb%d" % par
                    nb_, nbk = onb[par], "onb%d" % par
                    S.op("dve", lambda e: e.reciprocal(out=rs_[:, 0:1], in_=p0[:, 256:257]), reads=[k0], writes=[rk])
                    S.op("dve", lambda e: e.reciprocal(out=rs_[:, 1:2], in_=p1[:, 256:257]), reads=[k1], writes=[rk])
                    S.op("dve", lambda e: e.tensor_tensor(out=rs_[:, 2:3], in0=rs_[:, 1:2], in1=lsc[:, 4:5], op=ALU.mult),
                         reads=[rk, "lsc"], writes=[rk])
                    S.op("dve", lambda e: e.tensor_scalar(out=ob[:], in0=p0[:, 0:256], scalar1=rs_[:, 0:1], scalar2=None,
                                                           op0=ALU.mult), reads=[k0, rk], writes=[obk])
                    S.op("dve", lambda e: e.scalar_tensor_tensor(out=ob[:], in0=p1[:, 0:256], scalar=rs_[:, 2:3],
                                                                  in1=ob[:], op0=ALU.mult, op1=ALU.add),
                         reads=[k1, rk, obk], writes=[obk])
                    S.op("act", lambda e: e.activation(out=ojk[:], in_=ob[:], func=AF.Square, accum_out=rs_[:, 3:4]),
                         reads=[obk], writes=["ojk", rk])
                    S.op("dve", lambda e: e.tensor_scalar(out=rs_[:, 4:5], in0=rs_[:, 3:4], scalar1=1.0 / 256,
                                                           scalar2=SUBLN_EPS, op0=ALU.mult, op1=ALU.add),
                         reads=[rk], writes=[rk])
                    S.op("act", lambda e: e.activation(out=rs_[:, 5:6], in_=rs_[:, 4:5], func=AF.Ln), reads=[rk], writes=[rk])
                    S.op("act", lambda e: e.activation(out=rs_[:, 6:7], in_=rs_[:, 5:6], func=AF.Exp, scale=-0.5),
                         reads=[rk], writes=[rk])
                    S.op("dve", lambda e: e.scalar_tensor_tensor(out=nb_[:], in0=ob[:], scalar=rs_[:, 6:7], in1=subg[:],
                                                                  op0=ALU.mult, op1=ALU.mult),
                         reads=[obk, rk, "subg"], writes=[nbk])
                    ptile = banks[3][:].bitcast(BF16)[:, 0:256].rearrange("p (a b) -> p a b", b=128)
                    for dv in range(2):
                        S.op("pe", lambda e, dv=dv: e.transpose(ptile[:, dv, :], nb_[:, dv * 128:(dv + 1) * 128], ident[:]),
                             reads=[nbk, "ident"], writes=[bk[3]])
                    S.op("dve", lambda e: e.tensor_tensor(out=OG[:, :, qs], in0=ptile, in1=SG[:, :, qs], op=ALU.mult),
                         reads=[bk[3], kS], writes=[kO])

                ep_q = deque()
                for i in range(n + LAG):
                    if i < n:
                        scores(i)
                        S.pump(ep_q, 3)
                    if i - LAG >= 0:
                        pv(i - LAG)
                S.pump(ep_q, len(ep_q))
                S.dma("sp", OGd[s, 2 * h:2 * h + 2].rearrange("c p t -> p c t"), OG[:], reads=[kO],
                      writes=["OGd%d" % S.ninst])

            ih = 0
            for s in range(NSEQ):
                for h in range(HB):
                    attn_head(s, h, ih % 2)
                    ih += 1
        ck(12)
        S.barrier()
        with ExitStack() as sb3:
            OGt = [sb("OGt%d" % i, [128, NC, TT], BF16, sb3) for i in range(2)]
            lnbs = [outproj_ln(sb3, "B0"), outproj_ln(sb3, "B1")]
            it = 0
            for s in range(NSEQ):
                for j in range(NT):
                    t0 = j * TT
                    og = OGt[it % 2]
                    ogk = "OGt%d" % (it % 2)
                    it += 1
                    S.dma("sp", og[:], OGd[s][:, :, t0:t0 + TT].rearrange("c p t -> p c t"), reads=["OGd"], writes=[ogk])
                    run_outproj_ln(lnbs[(it - 1) % 2], W_BOUT, og, ogk, x1f[s][:, t0:t0 + TT], outT[s][:, t0:t0 + TT],
                                   V_LNG1, V_LNB1, "B%d" % ((it - 1) % 2), boff=4 * ((it - 1) % 2))
        S.barrier()


def _consts(C, TT):
    HB = C // 256
    i = np.arange(64)
    mstrict = (i[:, None] < i[None, :]).astype(np.float32)
    mlow = (i[:, None] > i[None, :]).astype(np.float32)
    mincl = (i[:, None] <= i[None, :]).astype(np.float32)
    m5 = np.stack([np.stack([mm_] * 4, axis=0) for mm_ in (mstrict, mlow, mincl)], axis=0)
    m5 = np.ascontiguousarray(m5.transpose(2, 0, 1, 3)).reshape(64, 3 * 4 * 64)
    m5 = np.concatenate([m5, m5], axis=0)
    idm = np.concatenate([np.eye(64, dtype=np.float32)] * 4, axis=1)
    idm = np.concatenate([idm, idm], axis=0)
    bones = np.kron(np.eye(2, dtype=np.float32), np.ones((64, 64), np.float32))
    onesc = np.full((128, 128), 1.0 / C, np.float32)
    rmask = np.ones((128, TT), np.float32)
    rmask[:, ::L] = 0.0
    p = np.arange(128)
    ual = np.zeros((128, HB, 4, 128), np.float32)
    dal = np.zeros((128, HB, 4, 128), np.float32)
    cb = np.zeros((128, HB, 4), np.float32)
    allowed = (p[:, None] // 64) <= (p[None, :] // 64)
    for h in range(HB):
        slope = 2.0 ** (-(8.0 / HB) * (h + 1))
        for j in range(4):
            ual[:, h, j, :] = -slope * ((3 - j) * 128 + p[None, :] - p[:, None])
            dal[:, h, j, :] = ual[:, h, j, :]
            cb[:, h, j] = -slope * 512.0 * j
        dal[:, h, 3, :] = np.where(allowed, -slope * np.abs(p[None, :] - p[:, None]), -30000.0)
    return dict(identd=np.eye(128, dtype=np.float32), m5d=m5, idmd=idm, bonesd=bones, onescd=onesc,
                rmaskd=rmask, ualid=ual.reshape(128, HB * 512), dalid=dal.reshape(128, HB * 512),
                slpd=cb.reshape(128, HB * 4))


def _params(inp, C):
    NC = C // 128

    def fm_w(W):
        return np.ascontiguousarray(W.reshape(NC, 128, NC, 128).transpose(2, 1, 0, 3).reshape(NC, 128, NC * 128))

    def fm_v(V):
        n = V.shape[0]
        return np.ascontiguousarray(V.reshape(n, NC, 128).transpose(2, 0, 1).reshape(128, n * NC))

    f = lambda k: np.asarray(inp[k], np.float32)
    win = f("a_w_in")[0]
    wqg = f("b_w_qg")[0]
    mats = [win[0], win[1], win[2], win[3], f("a_w_out")[0], f("w_k_shared"), f("w_v_shared"),
            wqg[:, :C], wqg[:, C:], f("b_w_out")[0]]
    wall = np.stack([fm_w(m) for m in mats], axis=0)
    lora_dn = lambda W: np.ascontiguousarray(W.reshape(NC, 128, LORA).transpose(1, 0, 2).reshape(128, NC * LORA))
    vecs = np.stack([f("a_w0")[0], f("a_a0")[0], f("a_k_k")[0], f("a_k_a")[0], f("a_r_k")[0].reshape(C),
                     f("a_gn_g")[0], f("a_gn_b")[0], f("ln_g")[0], f("ln_b")[0], f("ln_g")[1], f("ln_b")[1]], axis=0)
    mu6 = np.concatenate([f("a_mu_proj")[0], f("a_mu_lora")[0]], axis=0)
    return dict(wall=wall, w1l=lora_dn(f("a_w1")[0]), a1l=lora_dn(f("a_a1")[0]),
                lw2d=np.ascontiguousarray(np.stack([f("a_w2")[0].reshape(LORA, NC, 128), f("a_a2")[0].reshape(LORA, NC, 128)],
                                                   axis=2).transpose(1, 0, 2, 3).reshape(NC, LORA, 256)),
                mud=fm_v(mu6), vecd=fm_v(vecs),
                lamd=np.ascontiguousarray(np.broadcast_to(f("b_lambda")[0].reshape(1, 512), (128, 512))),
                subgd=np.ascontiguousarray(np.broadcast_to(f("b_subln_g")[0].reshape(1, 256), (128, 256))))


def run(inp, n_cores, TT, debug=False, runner=None, stop=0):
    x = np.asarray(inp["x"], np.float32)
    B, T, C = x.shape
    NSEQ = B // n_cores
    nc = build(C, T, NSEQ, TT, debug=debug, stop=stop)
    common = dict(_consts(C, TT))
    common.update(_params(inp, C))
    in_maps = []
    for i in range(n_cores):
        d = dict(common)
        d["xT"] = np.ascontiguousarray(x[i * NSEQ:(i + 1) * NSEQ].transpose(0, 2, 1))
        in_maps.append(d)
    if runner is None:
        res = run_bass_kernel_spmd(nc, in_maps, core_ids=list(range(n_cores))).results
    else:
        res = runner(nc, in_maps)
    out = np.concatenate([np.asarray(r["outT"]).transpose(0, 2, 1) for r in res], axis=0)
    if debug:
        x1 = np.concatenate([np.asarray(r["x1f"]).transpose(0, 2, 1) for r in res], axis=0)
        dbg = np.concatenate([np.asarray(r["dbg"]).transpose(0, 2, 1) for r in res], axis=0)
        return np.ascontiguousarray(out), np.ascontiguousarray(x1), dbg
    return np.ascontiguousarray(out.astype(np.float32))


def kernel(**inputs):
    return run(inputs, 8, 512)
```

```python
import math
from collections import deque
import numpy as np
from contextlib import ExitStack
import concourse.bass as bass
import concourse.mybir as mybir
from concourse.bass_utils import run_bass_kernel_spmd

F32 = mybir.dt.float32
BF16 = mybir.dt.bfloat16
AF = mybir.ActivationFunctionType
ALU = mybir.AluOpType
AX = mybir.AxisListType

ENGS = ["pe", "dve", "act", "pool", "sp"]
C0 = 0.6065306597126334
GN_EPS = 64e-5
LN_EPS = 1e-5
SUBLN_EPS = 1e-5
ALPHA = 4.0 ** 0.25
LAM_INIT = 0.8 - 0.6 * math.exp(-0.3 * 1)
LORA = 96
L = 64
SAME_ENG_WINDOW = 12
PUMP_MIN = 4.0
FRONT_RATIO = 1.0


class Sched:
    def __init__(self, nc, stack, ndma=48):
        self.nc = nc
        self.streams = {e: [] for e in ENGS}
        self.esem = {e: stack.enter_context(nc.semaphore("es_" + e)) for e in ENGS}
        self.ecnt = {e: 0 for e in ENGS}
        self.seen = {e: {} for e in ENGS}
        self.dsem = [stack.enter_context(nc.semaphore("ds%d" % i)) for i in range(ndma)]
        self.dcnt = [0] * ndma
        self.dnext = {"pool": 0, "hw": ndma // 2}
        self.drange = {"pool": (0, ndma // 2), "hw": (ndma // 2, ndma)}
        self.W = {}
        self.R = {}
        self.ninst = 0
        self.dead = False
        self.defer = None
        self.inter = None
        self._acc = 0.0
        self._pumping = False
        self.guard = None
        import os as _os
        self.maxi = int(_os.environ["FRONT_MAXI"]) if "FRONT_MAXI" in _os.environ else None
        self.npump = 0

    def _need(self, eng, deps):
        need = {}
        for tok in deps:
            if tok is None:
                continue
            key, h, val, src, seq = tok
            if src == eng:
                if eng == "pe":
                    continue
                if self.ecnt[eng] - seq >= SAME_ENG_WINDOW:
                    continue
            if self.seen[eng].get(key, 0) >= val:
                continue
            if key not in need or need[key][1] < val:
                need[key] = (h, val)
        return need

    def _collect(self, reads, writes):
        deps = []
        for r in reads:
            deps.append(self.W.get(r))
            if r.startswith("bank"):
                deps.extend(self.R.get(r, {}).values())
        for w in writes:
            deps.append(self.W.get(w))
            deps.extend(self.R.get(w, {}).values())
        return deps

    def _emit_waits(self, eng, need):
        for key, (h, val) in need.items():
            self.seen[eng][key] = val
            self.streams[eng].append(lambda e, h=h, val=val: e.wait_ge(h, val))
            self.ninst += 1

    def _record(self, tok, reads, writes):
        for r in reads:
            self.R.setdefault(r, {})[tok[0]] = tok
        for w in writes:
            self.W[w] = tok
            self.R[w] = {}

    def pump(self, lst, n):
        d, self.defer = self.defer, None
        p, self._pumping = self._pumping, True
        for _ in range(n):
            if not lst:
                break
            it = lst.popleft()
            if it[0] == "op":
                self.op(*it[1:])
            else:
                self.dma(*it[1:])
        self.defer = d
        self._pumping = p

    def _tick(self, eng=None):
        if self.inter is None or self._pumping:
            return
        lst, ratio = self.inter
        self._acc += ratio
        if self._acc < 1.0 or not lst:
            return
        if self.guard is not None and len(lst) <= self.guard:
            return
        budget = self._acc
        n = 0
        in_pe_run = False
        while lst:
            it = lst[0]
            ieng = it[1]
            if ieng == eng and not in_pe_run:
                break
            if budget < 1.0 and not (in_pe_run and ieng in ("pe", "pool")):
                break
            if self.guard is not None and len(lst) <= self.guard:
                break
            self.pump(lst, 1)
            in_pe_run = ieng in ("pe", "pool")
            budget -= 1.0
            n += 1
        self._acc = budget

    def op(self, eng, fn, reads=(), writes=()):
        if self.dead:
            return None
        if self.defer is not None:
            self.defer.append(("op", eng, fn, tuple(reads), tuple(writes)))
            return None
        need = self._need(eng, self._collect(reads, writes))
        self._emit_waits(eng, need)
        self.ecnt[eng] += 1
        h = self.esem[eng]
        self.streams[eng].append(lambda e, fn=fn, h=h: fn(e).then_inc(h, 1))
        tok = ("e_" + eng, h, self.ecnt[eng], eng, self.ecnt[eng])
        self._record(tok, reads, writes)
        self.ninst += 1
        self._tick(eng)
        return tok

    def dma(self, eng, out, in_, reads=(), writes=()):
        if self.dead:
            return None
        if self.defer is not None:
            self.defer.append(("dma", eng, out, in_, tuple(reads), tuple(writes)))
            return None
        grp = "pool" if eng == "pool" else "hw"
        k = self.dnext[grp]
        lo, hi = self.drange[grp]
        self.dnext[grp] = lo + (k + 1 - lo) % (hi - lo)
        deps = self._collect(reads, writes)
        key = "d%d" % k
        if self.dcnt[k] > 0:
            deps.append((key, self.dsem[k], self.dcnt[k], "dma", 0))
        need = self._need(eng, deps)
        self._emit_waits(eng, need)
        self.dcnt[k] += 16
        h = self.dsem[k]
        self.streams[eng].append(
            lambda e, out=out, in_=in_, h=h: e.dma_start(out=out, in_=in_).then_inc(h, 16))
        tok = (key, h, self.dcnt[k], "dma", 0)
        self._record(tok, reads, writes)
        self.ninst += 1
        return tok

    def link(self, src, dst):
        toks = {}
        for k in src:
            t = self.W.get(k)
            if t is not None and (t[0] not in toks or toks[t[0]][2] < t[2]):
                toks[t[0]] = t
            for t in self.R.get(k, {}).values():
                if t[0] not in toks or toks[t[0]][2] < t[2]:
                    toks[t[0]] = t
        for k in dst:
            d = self.R.setdefault(k, {})
            for sk, t in toks.items():
                if sk not in d or d[sk][2] < t[2]:
                    d[sk] = t

    def wait_all(self, eng, regions):
        need = self._need(eng, [self.W.get(r) for r in regions])
        self._emit_waits(eng, need)

    def barrier(self):
        if self.dead:
            return
        toks = []
        for e in ENGS:
            if self.ecnt[e] > 0:
                toks.append(("e_" + e, self.esem[e], self.ecnt[e], "bar", 0))
        for k in range(len(self.dsem)):
            if self.dcnt[k] > 0:
                toks.append(("d%d" % k, self.dsem[k], self.dcnt[k], "dma", 0))
        for e in ENGS:
            need = self._need(e, [t for t in toks if t[0] != "e_" + e])
            self._emit_waits(e, need)
        self.W = {}
        self.R = {}

    def finish(self):
        with self.nc.Block() as block:
            @block.tensor
            def _(e):
                for f in self.streams["pe"]:
                    f(e)

            @block.vector
            def _(e):
                for f in self.streams["dve"]:
                    f(e)

            @block.scalar
            def _(e):
                for f in self.streams["act"]:
                    f(e)

            @block.gpsimd
            def _(e):
                for f in self.streams["pool"]:
                    f(e)

            @block.sync
            def _(e):
                for f in self.streams["sp"]:
                    f(e)


W_R, W_K, W_V, W_G, W_AOUT, W_BK, W_BV, W_BQ, W_BG, W_BOUT = range(10)
V_W0, V_A0, V_KK, V_KA, V_RK, V_GNG, V_GNB, V_LNG0, V_LNB0, V_LNG1, V_LNB1 = range(11)
NV = 11


class _Stop(Exception):
    pass


def build(C, T, NSEQ, TT, debug=False, stop=0):
    NC = C // 128
    HB = C // 256
    NT = T // TT
    NCH = TT // L
    NB = TT // 128
    NQB = T // 128
    nc = bass.Bass("TRN2", target_bir_lowering=False)

    def din(name, shape):
        return nc.dram_tensor(name, list(shape), F32, kind="ExternalInput").ap()

    xT = din("xT", [NSEQ, C, T])
    wall = din("wall", [10, NC, 128, NC * 128])
    w1l = din("w1l", [128, NC * LORA])
    a1l = din("a1l", [128, NC * LORA])
    lw2d = din("lw2d", [NC, LORA, 256])
    mud = din("mud", [128, 6 * NC])
    vecd = din("vecd", [128, NV * NC])
    lamd = din("lamd", [128, 512])
    subgd = din("subgd", [128, 256])
    identd = din("identd", [128, 128])
    m5d = din("m5d", [128, 3 * 4 * 64])
    idmd = din("idmd", [128, 4 * 64])
    bonesd = din("bonesd", [128, 128])
    onescd = din("onescd", [128, 128])
    rmaskd = din("rmaskd", [128, TT])
    ualid = din("ualid", [128, HB * 512])
    dalid = din("dalid", [128, HB * 512])
    slpd = din("slpd", [128, HB * 4])
    outT = nc.dram_tensor("outT", [NSEQ, C, T], F32, kind="ExternalOutput").ap()
    if debug:
        x1f = nc.dram_tensor("x1f", [NSEQ, C, T], F32, kind="ExternalOutput").ap()
    else:
        x1f = nc.dram_tensor("x1f", [NSEQ, C, T], F32).ap()
    dbg = nc.dram_tensor("dbg", [NSEQ, C, T], F32, kind="ExternalOutput").ap() if debug else None
    KTd = nc.dram_tensor("KTd", [NSEQ, NC, 128, T], BF16).ap()
    QTd = nc.dram_tensor("QTd", [NSEQ, NC, 128, T], BF16).ap()
    SGd = nc.dram_tensor("SGd", [NSEQ, NC, 128, T], BF16).ap()
    OGd = nc.dram_tensor("OGd", [NSEQ, NC, 128, T], BF16).ap()
    VTd = nc.dram_tensor("VTd", [NSEQ, NQB, 128, C], BF16).ap()

    def ck(k):
        if stop == k:
            S.dead = True

    with ExitStack() as st:
        S = Sched(nc, st)
        _body(nc, S, st, locals())
        S.dead = False
        S.barrier()
        S.finish()
    return nc


def _body(nc, S, st, env):
    globals().update({})
    (C, T, NSEQ, TT, debug, NC, HB, NT, NCH, NB, NQB, ck) = (env[k] for k in
        ("C", "T", "NSEQ", "TT", "debug", "NC", "HB", "NT", "NCH", "NB", "NQB", "ck"))
    (xT, wall, w1l, a1l, lw2d, mud, vecd, lamd, subgd, identd, m5d, idmd, bonesd, onescd, rmaskd,
     ualid, dalid, slpd, outT, x1f, dbg, KTd, QTd, SGd, OGd, VTd) = (env[k] for k in
        ("xT", "wall", "w1l", "a1l", "lw2d", "mud", "vecd", "lamd", "subgd", "identd", "m5d", "idmd",
         "bonesd", "onescd", "rmaskd", "ualid", "dalid", "slpd", "outT", "x1f", "dbg", "KTd", "QTd", "SGd", "OGd", "VTd"))
    if True:

        def sb(name, shape, dt, stack=st):
            return stack.enter_context(nc.sbuf_tensor(name, list(shape), dt))

        ident = sb("ident", [128, 128], BF16)
        m3 = sb("m3", [128, 3, 4, 64], BF16)
        idm4 = sb("idm4", [128, 4, 64], BF16)
        bones = sb("bones", [128, 128], BF16)
        onesc = sb("onesc", [128, 128], BF16)
        vec = sb("vec", [128, NV, NC], F32)
        omka = sb("omka", [128, NC], F32)
        for nm, t, d in [("ident", ident, identd), ("m3", m3[:].rearrange("p a b c -> p (a b c)"), m5d), ("idm4", idm4[:].rearrange("p a b -> p (a b)"), idmd),
                         ("bones", bones, bonesd), ("onesc", onesc, onescd)]:
            S.dma("pool", t if nm in ("m3", "idm4") else t[:], d, writes=[nm])
        S.dma("sp", vec[:].rearrange("p a b -> p (a b)"), vecd, writes=["vec"])
        S.op("dve", lambda e: e.tensor_scalar(out=omka[:], in0=vec[:, V_KA, :], scalar1=-1.0, scalar2=1.0,
                                               op0=ALU.mult, op1=ALU.add), reads=["vec"], writes=["omka"])
        negv = sb("negv", [128, 2, NC], F32)
        S.op("dve", lambda e: e.tensor_scalar(out=negv[:], in0=vec[:, V_W0:V_A0 + 1, :], scalar1=-1.0, scalar2=None,
                                               op0=ALU.mult), reads=["vec"], writes=["negv"])
        ck(1)

        banks = [st.enter_context(nc.psum_tensor("bank%d" % i, [128, 512], F32)) for i in range(8)]
        bk = ["bank%d" % i for i in range(8)]

        NSLOT = 5
        wslots = [sb("wslot%d" % i, [128, NC, 128], BF16) for i in range(NSLOT)]
        worder = []
        for s in range(NSEQ):
            for j in range(NT):
                worder.append(("w1", 0))
                worder.append(("a1", 0))
                for c in range(NC):
                    for m in (W_V, W_G, W_R, W_K):
                        worder.append((m, c))
                for c in range(NC):
                    worder.append((W_AOUT, c))
        for s in range(NSEQ):
            for j in range(NT):
                for m in (W_BK, W_BV, W_BQ, W_BG):
                    for c in range(NC):
                        worder.append((m, c))
        for s in range(NSEQ):
            for j in range(NT):
                for c in range(NC):
                    worder.append((W_BOUT, c))
        wstate = {"issued": 0, "used": 0}
        PF = 4

        def wissue_upto(n):
            while wstate["issued"] < min(n, len(worder)):
                i = wstate["issued"]
                m, c = worder[i]
                sl = i % NSLOT
                if m in ("w1", "a1"):
                    S.dma("pool", wslots[sl][:].rearrange("p a b -> p (a b)")[:, 0:NC * LORA], w1l if m == "w1" else a1l,
                          writes=["wslot%d" % sl])
                else:
                    S.dma("pool", wslots[sl][:].rearrange("p a b -> p (a b)"), wall[m, c], writes=["wslot%d" % sl])
                wstate["issued"] += 1

        def wnext(m, c):
            i = wstate["used"]
            assert worder[i] == (m, c), (worder[i], m, c)
            wissue_upto(i + 1 + PF)
            wstate["used"] += 1
            return wslots[i % NSLOT], "wslot%d" % (i % NSLOT)

        def fm_proj(pbank, pkey, wt, wkey, rhs_fn, rkey):
            for dc in range(NC):
                S.op("pe", lambda e, dc=dc: e.matmul(pbank, lhsT=wt[:, dc, :], rhs=rhs_fn(dc),
                                                      start=(dc == 0), stop=(dc == NC - 1)),
                     reads=[wkey, rkey], writes=[pkey])

        def outproj_ln(stk, tagp):
            Z = sb(tagp + "Z", [128, NC, TT], F32, stk)
            mk = lambda nm, n, dt: [(sb(tagp + nm + str(i), [128, TT], dt, stk), tagp + nm + str(i)) for i in range(n)]
            return dict(Z=Z, xst=mk("xs", 2, F32), zb=mk("zb", 2, BF16), zq=mk("zq", 2, BF16),
                        lt=mk("lt", 4, F32), ost=mk("os", 2, F32))

        def run_outproj_ln(bufs, wm, src, srckey, resid_dram, dst_dram, vg, vb, tagp, boff=0):
            Z = bufs["Z"]
            xst = [t for t, _ in bufs["xst"]]
            xstk = [k for _, k in bufs["xst"]]
            zb = [t for t, _ in bufs["zb"]]
            zbk = [k for _, k in bufs["zb"]]
            zq = [t for t, _ in bufs["zq"]]
            zqk = [k for _, k in bufs["zq"]]
            lt = [t for t, _ in bufs["lt"]]
            kn = [k for _, k in bufs["lt"]]
            ost = [t for t, _ in bufs["ost"]]
            ostk = [k for _, k in bufs["ost"]]
            zk = tagp + "Z"
            pend = []

            def stats_mm(c, i2):
                S.op("pe", lambda e: e.matmul(banks[boff + 2][:, 0:TT], lhsT=onesc[:], rhs=zb[i2][:],
                                              start=(c == 0), stop=(c == NC - 1)),
                     reads=["onesc", zbk[i2]], writes=[bk[boff + 2]])
                S.op("pe", lambda e: e.matmul(banks[boff + 3][:, 0:TT], lhsT=onesc[:], rhs=zq[i2][:],
                                              start=(c == 0), stop=(c == NC - 1)),
                     reads=["onesc", zqk[i2]], writes=[bk[boff + 3]])

            for c in range(NC):
                wt, wkey = wnext(wm, c)
                pb = banks[boff + c % 2]
                pk = bk[boff + c % 2]
                fm_proj(pb[:, 0:TT], pk, wt, wkey, lambda dc: src[:, dc, :], srckey)
                while pend:
                    stats_mm(*pend.pop(0))
                i2 = c % 2
                xk = xstk[i2]
                S.dma("sp", xst[i2][:], resid_dram[c * 128:(c + 1) * 128, :], writes=[xk])
                S.op("dve", lambda e, c=c, pb=pb, i2=i2: e.scalar_tensor_tensor(
                    out=Z[:, c, :], in0=xst[i2][:], scalar=ALPHA, in1=pb[:, 0:TT], op0=ALU.mult, op1=ALU.add),
                    reads=[xk, pk], writes=[zk + str(c)])
                S.op("act", lambda e, c=c, i2=i2: e.activation(out=zb[i2][:], in_=Z[:, c, :], func=AF.Copy),
                     reads=[zk + str(c)], writes=[zbk[i2]])
                S.op("act", lambda e, c=c, i2=i2: e.activation(out=zq[i2][:], in_=Z[:, c, :], func=AF.Square),
                     reads=[zk + str(c)], writes=[zqk[i2]])
                pend.append((c, i2))
            while pend:
                stats_mm(*pend.pop(0))
            mean, msq, rstd, nmr = lt
            S.op("act", lambda e: e.activation(out=mean[:], in_=banks[boff + 2][:, 0:TT], func=AF.Copy),
                 reads=[bk[boff + 2]], writes=[kn[0]])
            S.op("act", lambda e: e.activation(out=msq[:], in_=banks[boff + 2][:, 0:TT], func=AF.Square),
                 reads=[bk[boff + 2]], writes=[kn[1]])
            S.op("dve", lambda e: e.tensor_tensor(out=rstd[:], in0=banks[boff + 3][:, 0:TT], in1=msq[:], op=ALU.subtract),
                 reads=[bk[boff + 3], kn[1]], writes=[kn[2]])
            S.op("dve", lambda e: e.tensor_scalar(out=rstd[:], in0=rstd[:], scalar1=LN_EPS, scalar2=None,
                                                   op0=ALU.add), reads=[kn[2]], writes=[kn[2]])
            S.op("act", lambda e: e.activation(out=rstd[:], in_=rstd[:], func=AF.Ln), reads=[kn[2]], writes=[kn[2]])
            S.op("act", lambda e: e.activation(out=rstd[:], in_=rstd[:], func=AF.Exp, scale=-0.5),
                 reads=[kn[2]], writes=[kn[2]])
            S.op("dve", lambda e: e.scalar_tensor_tensor(out=nmr[:], in0=mean[:], scalar=-1.0, in1=rstd[:],
                                                          op0=ALU.mult, op1=ALU.mult),
                 reads=[kn[0], kn[2]], writes=[kn[3]])
            for c in range(NC):
                i2 = c % 2
                ok = ostk[i2]
                S.op("dve", lambda e, c=c: e.tensor_tensor(out=Z[:, c, :], in0=Z[:, c, :], in1=rstd[:], op=ALU.mult),
                     reads=[zk + str(c), kn[2]], writes=[zk + str(c)])
                S.op("dve", lambda e, c=c: e.tensor_tensor(out=Z[:, c, :], in0=Z[:, c, :], in1=nmr[:], op=ALU.add),
                     reads=[zk + str(c), kn[3]], writes=[zk + str(c)])
                S.op("act", lambda e, c=c, i2=i2: e.activation(out=ost[i2][:], in_=Z[:, c, :], func=AF.Identity,
                                                                scale=vec[:, vg, c:c + 1], bias=vec[:, vb, c:c + 1]),
                     reads=[zk + str(c), "vec"], writes=[ok])
                S.dma("sp", dst_dram[c * 128:(c + 1) * 128, :], ost[i2][:], reads=[ok], writes=["dst%s%d_%d" % (tagp, c, S.ninst)])

        with ExitStack() as sa:
            mu = sb("mu", [128, 6, NC], F32, sa)
            lwt = [sb("lwt%d" % i, [LORA, 2, 128], BF16, sa) for i in range(2)]
            rmask = sb("rmask", [128, TT], BF16, sa)
            S.dma("sp", mu[:].rearrange("p a b -> p (a b)"), mud, writes=["mu"])
            S.dma("pool", rmask[:], rmaskd, writes=["rmask"])
            xs = [sb("xs%d" % i, [128, NC, TT], BF16, sa) for i in range(4)]
            ZA = sb("AZ", [128, NC, TT], F32, sa)
            zbf = ZA[:].rearrange("p a b -> p (a b)").bitcast(BF16)
            xs.append(zbf[:, 0:NC * TT].rearrange("p (a b) -> p a b", b=TT))
            xs.append(zbf[:, NC * TT:2 * NC * TT].rearrange("p (a b) -> p a b", b=TT))
            xs = [x if i >= 4 else x[:] for i, x in enumerate(xs)]
            YG = sb("YG", [128, NC, TT], BF16, sa)
            xstage = [sb("xstage%d" % i, [128, TT + 1], F32, sa) for i in range(1)] * 2
            hw_b = sb("hw_b", [LORA, TT], BF16, sa)
            ha_b = sb("ha_b", [LORA, TT], BF16, sa)
            f = [sb("f%d" % i, [128, TT], F32, sa) for i in range(10)]
            b = [sb("b%d" % i, [128, TT], BF16, sa) for i in range(8)]
            fk = ["f%d" % i for i in range(10)]
            bkk = ["b%d" % i for i in range(8)]
            TMs = [[sb("TM%s%d" % (nm, i), [128, NCH, 64], BF16, sa) for nm in "bkvp"] for i in range(2)]
            b4s = [b[4], sb("b4x", [128, TT], BF16, sa)]
            b4k = [bkk[4], "b4x"]
            E1s = [sb("e3x%d" % i, [128, TT], F32, sa) for i in range(2)]
            E1k = ["e3x0", "e3x1"]
            E2s = [sb("e4x%d" % i, [128, TT], F32, sa) for i in range(2)]
            E2k = ["e4x0", "e4x1"]
            gnt = [sb("gnt%d" % i, [128, TT], F32, sa) for i in range(2)]
            xxs = [gnt[0]] * 2
            pLt = [sb("pLt%d" % i, [128, NCH], F32, sa) for i in range(2)]
            AM01 = sb("AM01", [128, 2, NCH, 64], BF16, sa)
            AM23 = sb("AM23", [128, 2, NCH, 64], BF16, sa)
            AM4 = sb("AM4", [128, NCH, 64], BF16, sa)
            XAB = [sb("XAB%d" % i, [128, 2, NCH, 64], BF16, sa) for i in range(2)]
            Tm = sb("Tm", [128, NCH, 64], BF16, sa)
            Sf = sb("Sf", [128, NC, 64], F32, sa)
            Sb_ = sb("Sb", [128, NC, 64], BF16, sa)
            lnbufs = dict(Z=ZA, xst=[(f[4], fk[4]), (f[5], fk[5])], zb=[(b[0], bkk[0]), (b[1], bkk[1])],
                          zq=[(b[2], bkk[2]), (b[3], bkk[3])], lt=[(f[i], fk[i]) for i in range(4)],
                          ost=[(f[6], fk[6]), (f[7], fk[7])])
            zkeys = ["AZ%d" % c for c in range(NC)]

            for s in range(NSEQ):
                S.op("dve", lambda e: e.memset(Sf[:], 0.0), writes=["Sf"])
                S.op("dve", lambda e: e.memset(Sb_[:], 0.0), writes=["Sb"])
                for j in range(NT):
                    t0 = j * TT
                    S.link(zkeys, ["xs4", "xs5"])
                    for dc in range(NC):
                        i2 = dc % 2
                        xk = "xstage0"
                        if j == 0:
                            S.op("dve", lambda e, i2=i2: e.memset(xstage[i2][:, 0:1], 0.0), writes=[xk])
                            S.dma("sp", xstage[i2][:, 1:TT + 1], xT[s, dc * 128:(dc + 1) * 128, 0:TT], writes=[xk])
                        else:
                            S.dma("sp", xstage[i2][:], xT[s, dc * 128:(dc + 1) * 128, t0 - 1:t0 + TT], writes=[xk])
                        S.op("dve", lambda e, i2=i2: e.tensor_tensor(out=xxs[i2][:], in0=xstage[i2][:, 0:TT],
                                                                      in1=xstage[i2][:, 1:TT + 1], op=ALU.subtract),
                             reads=[xk], writes=["gnt0"])
                        for i in range(6):
                            S.op("dve", lambda e, i=i, i2=i2, dc=dc: e.scalar_tensor_tensor(
                                out=xs[i][:, dc, :], in0=xxs[i2][:], scalar=mu[:, i, dc:dc + 1],
                                in1=xstage[i2][:, 1:TT + 1], op0=ALU.mult, op1=ALU.add),
                                reads=["gnt0", xk, "mu"], writes=["xs%d" % i])
                    ck(2)
                    for (wnm, xi, hb, hk, fn) in (("w1", 4, hw_b, "hw_b", AF.Tanh), ("a1", 5, ha_b, "ha_b", AF.Copy)):
                        wt_, wk_ = wnext(wnm, 0)
                        wl = wt_[:].rearrange("p a b -> p (a b)")[:, 0:NC * LORA].rearrange("p (a b) -> p a b", b=LORA)
                        for dc in range(NC):
                            S.op("pe", lambda e, dc=dc, wl=wl, xi=xi: e.matmul(
                                banks[4][0:LORA, 0:TT], lhsT=wl[:, dc, :], rhs=xs[xi][:, dc, :],
                                start=(dc == 0), stop=(dc == NC - 1)),
                                reads=[wk_, "xs%d" % xi], writes=[bk[4]])
                        S.op("act", lambda e, hb=hb, fn=fn: e.activation(out=hb[:], in_=banks[4][0:LORA, 0:TT], func=fn),
                             reads=[bk[4]], writes=[hk])
                    ck(3)
                    def front(c, par, s=s, j=j, t0=t0):
                        pr, pk_, pv, pg = (banks[i][:, 0:TT] for i in range(4))
                        pw, pa = pr, pk_
                        kpw, kpa = bk[0], bk[1]
                        TMb, TMk, TMv, TMp = TMs[par]
                        kTM = ["TM%s%d" % (nm, par) for nm in "bkvp"]
                        lw_ = lwt[par]
                        S.dma("pool", lw_[:].rearrange("p a b -> p (a b)"), lw2d[c], writes=["lwt%d" % par])
                        S.op("pe", lambda e: e.matmul(pw, lhsT=lw_[:, 0, :], rhs=hw_b[:], start=True, stop=True),
                             reads=["lwt%d" % par, "hw_b"], writes=[kpw])
                        S.op("pe", lambda e: e.matmul(pa, lhsT=lw_[:, 1, :], rhs=ha_b[:], start=True, stop=True),
                             reads=["lwt%d" % par, "ha_b"], writes=[kpa])

                        def vc(i, c=c):
                            return vec[:, i, c:c + 1]
                        nw0 = negv[:, 0, c:c + 1]
                        na0 = negv[:, 1, c:c + 1]
                        S.op("act", lambda e: e.activation(out=f[0][:], in_=pw, func=AF.Exp, scale=-1.0, bias=nw0),
                             reads=[kpw, "negv"], writes=[fk[0]])
                        S.op("act", lambda e: e.activation(out=f[0][:], in_=f[0][:], func=AF.Ln, bias=1.0),
                             reads=[fk[0]], writes=[fk[0]])
                        S.op("act", lambda e: e.activation(out=f[0][:], in_=f[0][:], func=AF.Exp, scale=-1.0),
                             reads=[fk[0]], writes=[fk[0]])
                        S.op("act", lambda e: e.activation(out=f[4][:], in_=pa, func=AF.Exp, scale=-1.0, bias=na0),
                             reads=[kpa, "negv"], writes=[fk[4]])
                        S.op("act", lambda e: e.activation(out=f[4][:], in_=f[4][:], func=AF.Ln, bias=1.0),
                             reads=[fk[4]], writes=[fk[4]])
                        S.op("act", lambda e: e.activation(out=f[4][:], in_=f[4][:], func=AF.Exp, scale=-1.0),
                             reads=[fk[4]], writes=[fk[4]])
                        ck(41)
                        for m, xi, pbank, pkey in ((W_V, 2, pv, bk[2]), (W_G, 3, pg, bk[3]),
                                                   (W_R, 0, pr, bk[0]), (W_K, 1, pk_, bk[1])):
                            wt, wkey = wnext(m, c)
                            fm_proj(pbank, pkey, wt, wkey, lambda dc, xi=xi: xs[xi][:, dc, :], "xs%d" % xi)
                            ck(420 + xi)
                        S.op("act", lambda e: e.activation(out=f[9][:], in_=pg, func=AF.Exp, scale=-1.0),
                             reads=[bk[3]], writes=[fk[9]])
                        S.op("act", lambda e: e.activation(out=f[9][:], in_=f[9][:], func=AF.Ln, bias=1.0),
                             reads=[fk[9]], writes=[fk[9]])
                        S.op("act", lambda e: e.activation(out=f[9][:], in_=f[9][:], func=AF.Exp, scale=-1.0),
                             reads=[fk[9]], writes=[fk[9]])
                        S.op("dve", lambda e: e.tensor_tensor(out=f[9][:], in0=pg, in1=f[9][:], op=ALU.mult),
                             reads=[bk[3], fk[9]], writes=[fk[9]])
                        S.op("act", lambda e: e.activation(out=b[5][:], in_=pv, func=AF.Copy),
                             reads=[bk[2]], writes=[bkk[5]])
                        pss = banks[3][:, 0:TT]
                        pbs = banks[2][:, 0:TT]
                        ck(43)
                        S.op("dve", lambda e: e.tensor_tensor_scan(out=f[1][:], data0=rmask[:], data1=f[0][:],
                                                                    initial=0.0, op0=ALU.mult, op1=ALU.add),
                             reads=["rmask", fk[0]], writes=[fk[1]])
                        ck(501)
                        S.op("act", lambda e: e.activation(out=f[2][:], in_=f[1][:], func=AF.Exp, scale=-C0),
                             reads=[fk[1]], writes=[fk[2]])
                        ck(502)
                        S.op("act", lambda e: e.activation(out=f[3][:], in_=f[1][:], func=AF.Exp, scale=C0),
                             reads=[fk[1]], writes=[fk[3]])
                        ck(503)
                        S.op("dve", lambda e: e.tensor_tensor(out=f[0][:], in0=f[1][:], in1=f[0][:], op=ALU.subtract),
                             reads=[fk[1], fk[0]], writes=[fk[0]])
                        ck(504)
                        S.op("act", lambda e: e.activation(out=f[0][:], in_=f[0][:], func=AF.Exp, scale=-C0),
                             reads=[fk[0]], writes=[fk[0]])
                        ck(505)
                        S.op("dve", lambda e: e.tensor_scalar(out=f[5][:], in0=pk_, scalar1=vc(V_KK), scalar2=None,
                                                               op0=ALU.mult), reads=[bk[1], "vec"], writes=[fk[5]])
                        ck(506)
                        S.op("act", lambda e: e.activation(out=b[0][:], in_=pk_, func=AF.Square, scale=vc(V_KK)),
                             reads=[bk[1], "vec", fk[5]], writes=[bkk[0]])
                        ck(507)
                        S.op("pe", lambda e: e.matmul(pss, lhsT=bones[:], rhs=b[0][:], start=True, stop=True),
                             reads=["bones", bkk[0]], writes=[bk[3]])
                        ck(508)
                        S.op("dve", lambda e: e.tensor_scalar(out=f[6][:], in0=pss, scalar1=1e-19, scalar2=None, op0=ALU.max),
                             reads=[bk[3]], writes=[fk[6]])
                        S.op("act", lambda e: e.activation(out=f[6][:], in_=f[6][:], func=AF.Ln), reads=[fk[6]], writes=[fk[6]])
                        S.op("act", lambda e: e.activation(out=f[6][:], in_=f[6][:], func=AF.Exp, scale=-0.5),
                             reads=[fk[6]], writes=[fk[6]])
                        ck(509)
                        ck(511)
                        S.op("dve", lambda e: e.tensor_tensor(out=f[5][:], in0=f[5][:], in1=f[6][:], op=ALU.mult),
                             reads=[fk[5], fk[6]], writes=[fk[5]])
                        ck(512)
                        S.op("dve", lambda e: e.tensor_tensor(out=f[7][:], in0=f[5][:], in1=f[4][:], op=ALU.mult),
                             reads=[fk[5], fk[4]], writes=[fk[7]])
                        ck(513)
                        S.op("dve", lambda e, c=c: e.tensor_scalar(out=f[4][:], in0=f[4][:], scalar1=vc(V_KA),
                                                                    scalar2=omka[:, c:c + 1], op0=ALU.mult, op1=ALU.add),
                             reads=[fk[4], "vec", "omka"], writes=[fk[4]])
                        ck(514)
                        S.op("dve", lambda e: e.tensor_tensor(out=f[8][:], in0=pk_, in1=f[4][:], op=ALU.mult),
                             reads=[bk[1], fk[4]], writes=[fk[8]])
                        ck(515)
                        S.op("dve", lambda e: e.tensor_tensor(out=b[1][:], in0=f[5][:], in1=f[0][:], op=ALU.mult),
                             reads=[fk[5], fk[0]], writes=[bkk[1]])
                        ck(516)
                        S.op("dve", lambda e: e.tensor_tensor(out=b[2][:], in0=f[7][:], in1=f[3][:], op=ALU.mult),
                             reads=[fk[7], fk[3]], writes=[bkk[2]])
                        ck(517)
                        S.op("dve", lambda e: e.tensor_tensor(out=b[3][:], in0=f[8][:], in1=f[3][:], op=ALU.mult),
                             reads=[fk[8], fk[3]], writes=[bkk[3]])
                        ck(518)
                        S.op("dve", lambda e: e.tensor_tensor(out=b4s[par][:], in0=pr, in1=f[2][:], op=ALU.mult),
                             reads=[bk[0], fk[2]], writes=[b4k[par]])
                        ck(519)
                        S.op("act", lambda e: e.activation(
                            out=pLt[par][:], in_=f[2][:].rearrange("p (a b) -> p a b", b=L)[:, :, L - 1], func=AF.Copy),
                            reads=[fk[2]], writes=["pLt%d" % par])
                        ck(520)
                        S.op("dve", lambda e: e.tensor_tensor(out=f[6][:], in0=pr, in1=f[8][:], op=ALU.mult),
                             reads=[bk[0], fk[8]], writes=[fk[6]])
                        ck(521)
                        S.op("act", lambda e: e.activation(out=b[0][:], in_=f[6][:], func=AF.Copy, scale=vc(V_RK)),
                             reads=[fk[6], "vec"], writes=[bkk[0]])
                        ck(522)
                        S.op("pe", lambda e: e.matmul(pbs, lhsT=bones[:], rhs=b[0][:], start=True, stop=True),
                             reads=["bones", bkk[0]], writes=[bk[2]])
                        ck(523)
                        S.op("dve", lambda e: e.tensor_tensor(out=f[5][:], in0=pbs, in1=b[5][:], op=ALU.mult),
                             reads=[bk[2], bkk[5]], writes=[fk[5]])
                        ck(524)
                        S.op("dve", lambda e: e.tensor_scalar(out=E1s[par][:], in0=f[9][:], scalar1=vc(V_GNG), scalar2=None,
                                                               op0=ALU.mult), reads=[fk[9], "vec"], writes=[E1k[par]])
                        ck(525)
                        S.op("dve", lambda e: e.scalar_tensor_tensor(out=E2s[par][:], in0=f[5][:], scalar=vc(V_GNB),
                                                                      in1=f[9][:], op0=ALU.add, op1=ALU.mult),
                             reads=[fk[5], fk[9], "vec"], writes=[E2k[par]])
                        ck(526)
                        ck(4)
                        for ti, (src, skey, dstt, dkey) in enumerate(((b[2], bkk[2], TMb, kTM[0]), (b[3], bkk[3], TMk, kTM[1]),
                                                                      (b[5], bkk[5], TMv, kTM[2]), (b[1], bkk[1], TMp, kTM[3]))):
                            pbk = ti % 2
                            ptile = banks[pbk][:].bitcast(BF16)[:, 0:NCH * 64].rearrange("p (a b) -> p a b", b=64)
                            for cc in range(NCH):
                                for hh in range(2):
                                    hs = slice(hh * 64, hh * 64 + 64)
                                    S.op("pe", lambda e, src=src, cc=cc, ptile=ptile, hs=hs: e.transpose(
                                        ptile[hs, cc, :], src[hs, cc * L:(cc + 1) * L], ident[hs, hs]),
                                        reads=[skey, "ident"], writes=[bk[pbk]])
                            S.op("act", lambda e, dstt=dstt, ptile=ptile: e.activation(out=dstt[:], in_=ptile, func=AF.Copy),
                                 reads=[bk[pbk]], writes=[dkey])
                        ck(5)

                    def scan(c, par, s=s, j=j, t0=t0):
                        TMb, TMk, TMv, TMp = TMs[par]
                        kTM = ["TM%s%d" % (nm, par) for nm in "bkvp"]
                        r4, r4k = b4s[par], b4k[par]
                        GS = NCH // 2
                        grp = [slice(0, GS), slice(GS, NCH)]
                        W4 = GS * 64

                        def pv2(bank):
                            return banks[bank][:, 0:2 * W4].rearrange("p (t a b) -> p t a b", t=2, b=64)

                        def pv1(bank):
                            return banks[bank][:, 0:W4].rearrange("p (a b) -> p a b", b=64)

                        def hsl(hh):
                            return slice(hh * 64, hh * 64 + 64)

                        am_banks = [(4, 5, 6), (7, 4, 5)]
                        if S.inter is not None:
                            lst_ = S.inter[0]
                            nsafe = 0
                            for it_ in lst_:
                                if any(w in (bkk[1], bkk[2], bkk[3]) for w in it_[-1]):
                                    break
                                nsafe += 1
                            S.guard = len(lst_) - nsafe
                        for g in range(2):
                            bA, bB, bC = am_banks[g]
                            specs = ((bA, 0, b[2], b[1], bkk[2], bkk[1]),
                                     (bA, 1, b[1], b[2], bkk[1], bkk[2]),
                                     (bB, 0, b[1], b[3], bkk[1], bkk[3]),
                                     (bB, 1, b[2], r4, bkk[2], r4k),
                                     (bC, 0, b[3], r4, bkk[3], r4k))
                            for (bnk, half, lt_, rt_, lk, rk_) in specs:
                                pv_ = pv2(bnk)
                                for ci, cc in enumerate(range(g * GS, (g + 1) * GS)):
                                    cs = slice(cc * L, (cc + 1) * L)
                                    for hh in range(2):
                                        hs = hsl(hh)
                                        S.op("pe", lambda e, pv_=pv_, half=half, ci=ci, lt_=lt_, rt_=rt_, hs=hs, cs=cs: e.matmul(
                                            pv_[hs, half, ci, :], lhsT=lt_[hs, cs], rhs=rt_[hs, cs], start=True, stop=True),
                                            reads=[lk, rk_], writes=[bk[bnk]])
                            S.op("dve", lambda e, bA=bA, g=g: e.tensor_tensor(
                                out=AM01[:, :, grp[g], :], in0=pv2(bA), in1=m3[:, 0:2, 0:GS, :], op=ALU.mult),
                                reads=[bk[bA], "m3"], writes=["AM01" + str(g)])
                            S.op("dve", lambda e, bB=bB, g=g: e.tensor_tensor(
                                out=AM23[:, :, grp[g], :], in0=pv2(bB), in1=m3[:, 1:3, 0:GS, :], op=ALU.mult),
                                reads=[bk[bB], "m3"], writes=["AM23" + str(g)])
                            S.op("dve", lambda e, bC=bC, g=g: e.tensor_tensor(
                                out=AM4[:, grp[g], :], in0=pv1(bC), in1=m3[:, 2, 0:GS, :], op=ALU.mult),
                                reads=[bk[bC], "m3"], writes=["AM4" + str(g)])
                            S.op("dve", lambda e, g=g: e.tensor_tensor(out=Tm[:, grp[g], :], in0=idm4[:, 0:GS, :],
                                                                        in1=AM01[:, 0, grp[g], :], op=ALU.subtract),
                                 reads=["idm4", "AM01" + str(g)], writes=["Tm" + str(g)])
                        ck(6)
                        S.guard = None
                        Xc, Xck = AM01, "AM01"
                        for lev in range(1, 6):
                            Xn = XAB[lev % 2]
                            Xnk = "XAB%d" % (lev % 2)
                            for g in range(2):
                                bkx = 4 + g
                                px = pv2(bkx)
                                for ci, cc in enumerate(range(g * GS, (g + 1) * GS)):
                                    for hh in range(2):
                                        hs = hsl(hh)
                                        if lev < 5:
                                            S.op("pe", lambda e, px=px, ci=ci, cc=cc, hs=hs, Xc=Xc: e.matmul(
                                                px[hs, 0, ci, :], lhsT=Xc[hs, 1, cc, :], rhs=Xc[hs, 0, cc, :], start=True, stop=True),
                                                reads=[Xck + str(g)], writes=[bk[bkx]])
                                        S.op("pe", lambda e, px=px, ci=ci, cc=cc, hs=hs, Xc=Xc: e.matmul(
                                            px[hs, 1, ci, :], lhsT=Xc[hs, 0, cc, :], rhs=Xc[hs, 1, cc, :], start=True, stop=True),
                                            reads=[Xck + str(g)], writes=[bk[bkx]])
                                if lev < 5:
                                    S.op("act", lambda e, px=px, Xn=Xn, g=g: e.activation(
                                        out=Xn[:, :, grp[g], :], in_=px, func=AF.Copy),
                                        reads=[bk[bkx]], writes=[Xnk + str(g)])
                                else:
                                    S.op("act", lambda e, px=px, Xn=Xn, g=g: e.activation(
                                        out=Xn[:, 1, grp[g], :], in_=px[:, 1], func=AF.Copy),
                                        reads=[bk[bkx]], writes=[Xnk + str(g)])
                            for g in range(2):
                                bkt = 6 + g
                                ptp = pv1(bkt)
                                for ci, cc in enumerate(range(g * GS, (g + 1) * GS)):
                                    for hh in range(2):
                                        hs = hsl(hh)
                                        S.op("pe", lambda e, ptp=ptp, ci=ci, cc=cc, hs=hs, Xn=Xn: e.matmul(
                                            ptp[hs, ci, :], lhsT=Xn[hs, 1, cc, :], rhs=Tm[hs, cc, :], start=True, stop=True),
                                            reads=[Xnk + str(g), "Tm" + str(g)], writes=[bk[bkt]])
                                S.op("dve", lambda e, ptp=ptp, g=g: e.tensor_tensor(out=Tm[:, grp[g], :], in0=ptp,
                                                                                    in1=Tm[:, grp[g], :], op=ALU.add),
                                     reads=[bk[bkt], "Tm" + str(g)], writes=["Tm" + str(g)])
                            Xc, Xck = Xn, Xnk
                        ck(7)
                        for g in range(2):
                            bkx = 4 + g
                            pgv = pv2(bkx)
                            for ci, cc in enumerate(range(g * GS, (g + 1) * GS)):
                                for hh in range(2):
                                    hs = hsl(hh)
                                    S.op("pe", lambda e, pgv=pgv, ci=ci, cc=cc, hs=hs: e.matmul(
                                        pgv[hs, 0, ci, :], lhsT=Tm[hs, cc, :], rhs=TMp[hs, cc, :], start=True, stop=True),
                                        reads=["Tm" + str(g), kTM[3]], writes=[bk[bkx]])
                                    S.op("pe", lambda e, pgv=pgv, ci=ci, cc=cc, hs=hs: e.matmul(
                                        pgv[hs, 1, ci, :], lhsT=Tm[hs, cc, :], rhs=AM23[hs, 0, cc, :], start=True, stop=True),
                                        reads=["Tm" + str(g), "AM23" + str(g)], writes=[bk[bkx]])
                            S.op("act", lambda e, pgv=pgv, g=g: e.activation(out=AM01[:, :, grp[g], :], in_=pgv, func=AF.Copy),
                                 reads=[bk[bkx]], writes=["AM01" + str(g)])
                        for g in range(2):
                            b0_, b1_ = 4 + g, 6 + g
                            pDN, pRA = pv2(b0_), pv2(b1_)
                            k0, k1 = bk[b0_], bk[b1_]
                            for ci, cc in enumerate(range(g * GS, (g + 1) * GS)):
                                for hh in range(2):
                                    hs = hsl(hh)
                                    for li in range(2):
                                        S.op("pe", lambda e, pDN=pDN, ci=ci, cc=cc, hs=hs, li=li: e.matmul(
                                            pDN[hs, li, ci, :], lhsT=AM01[hs, li, cc, :], rhs=TMb[hs, cc, :], start=True, stop=True),
                                            reads=["AM01" + str(g), kTM[0]], writes=[k0])
                                    for li in range(2):
                                        S.op("pe", lambda e, pRA=pRA, ci=ci, cc=cc, hs=hs, li=li: e.matmul(
                                            pRA[hs, li, ci, :], lhsT=AM01[hs, li, cc, :], rhs=AM23[hs, 1, cc, :], start=True, stop=True),
                                            reads=["AM01" + str(g), "AM23" + str(g)], writes=[k1])
                            S.op("act", lambda e, pDN=pDN, g=g: e.activation(out=XAB[0][:, 0, grp[g], :], in_=pDN[:, 0],
                                                                               func=AF.Copy, scale=-1.0),
                                 reads=[k0], writes=["XAB0" + str(g)])
                            S.op("dve", lambda e, pDN=pDN, g=g: e.tensor_tensor(out=XAB[0][:, 1, grp[g], :], in0=TMk[:, grp[g], :],
                                                                                 in1=pDN[:, 1], op=ALU.subtract),
                                 reads=[k0, kTM[1]], writes=["XAB0" + str(g)])
                            S.op("dve", lambda e, pRA=pRA, g=g: e.tensor_tensor(
                                out=XAB[1][:, 0, grp[g], :],
                                in0=r4[:, g * W4:(g + 1) * W4].rearrange("p (a b) -> p a b", b=64), in1=pRA[:, 0], op=ALU.subtract),
                                reads=[k1, r4k], writes=["XAB1" + str(g)])
                            S.op("dve", lambda e, pRA=pRA, g=g: e.tensor_tensor(out=XAB[1][:, 1, grp[g], :], in0=AM4[:, grp[g], :],
                                                                                 in1=pRA[:, 1], op=ALU.subtract),
                                 reads=[k1, "AM4" + str(g)], writes=["XAB1" + str(g)])
                        py = banks[7]
                        for cc in range(NCH):
                            g = cc // GS
                            cs = slice(cc * L, (cc + 1) * L)
                            pS = banks[4 + (cc % 2)][:, 0:64]
                            pSk = bk[4 + (cc % 2)]
                            pl = pLt[par][:, cc:cc + 1]
                            plk = "pLt%d" % par
                            S.op("dve", lambda e, c=c, pl=pl: e.tensor_scalar(out=Sf[:, c, :], in0=Sf[:, c, :], scalar1=pl,
                                                                               scalar2=None, op0=ALU.mult),
                                 reads=["Sf", plk], writes=["Sf"])
                            for hh in range(2):
                                hs = hsl(hh)
                                S.op("pe", lambda e, hs=hs, cc=cc, c=c, pS=pS: e.matmul(
                                    pS[hs, :], lhsT=XAB[0][hs, 0, cc, :], rhs=Sb_[hs, c, :], start=True, stop=False),
                                    reads=["XAB0" + str(g), "Sb"], writes=[pSk])
                                S.op("pe", lambda e, hs=hs, cc=cc, pS=pS: e.matmul(
                                    pS[hs, :], lhsT=XAB[0][hs, 1, cc, :], rhs=TMv[hs, cc, :], start=False, stop=True),
                                    reads=["XAB0" + str(g), kTM[2]], writes=[pSk])
                            for hh in range(2):
                                hs = hsl(hh)
                                S.op("pe", lambda e, hs=hs, cs=cs, cc=cc, c=c: e.matmul(
                                    py[hs, cs], lhsT=Sb_[hs, c, :], rhs=XAB[1][hs, 0, cc, :], start=True, stop=False),
                                    reads=["Sb", "XAB1" + str(g)], writes=[bk[7]])
                                S.op("pe", lambda e, hs=hs, cs=cs, cc=cc: e.matmul(
                                    py[hs, cs], lhsT=TMv[hs, cc, :], rhs=XAB[1][hs, 1, cc, :], start=False, stop=True),
                                    reads=[kTM[2], "XAB1" + str(g)], writes=[bk[7]])
                            S.op("dve", lambda e, c=c, pl=pl, pS=pS: e.scalar_tensor_tensor(
                                out=Sb_[:, c, :], in0=pS, scalar=pl, in1=Sf[:, c, :], op0=ALU.mult, op1=ALU.add),
                                reads=[pSk, "Sf", plk], writes=["Sb"])
                            S.op("dve", lambda e, c=c, pl=pl, pS=pS: e.scalar_tensor_tensor(
                                out=Sf[:, c, :], in0=pS, scalar=pl, in1=Sf[:, c, :], op0=ALU.mult, op1=ALU.add),
                                reads=[pSk, "Sf", plk], writes=["Sf"])
                            ck(8)
                        pyv = banks[7][:, 0:TT]
                        g0, g1 = gnt
                        E1, E2 = E1s[par], E2s[par]
                        S.op("act", lambda e: e.activation(out=b[6][:], in_=pyv, func=AF.Copy), reads=[bk[7]], writes=[bkk[6]])
                        S.op("act", lambda e: e.activation(out=b[7][:], in_=pyv, func=AF.Square), reads=[bk[7]], writes=[bkk[7]])
                        S.op("pe", lambda e: e.matmul(banks[5][:, 0:TT], lhsT=bones[:], rhs=b[6][:], start=True, stop=True),
                             reads=["bones", bkk[6]], writes=[bk[5]])
                        S.op("pe", lambda e: e.matmul(banks[6][:, 0:TT], lhsT=bones[:], rhs=b[7][:], start=True, stop=True),
                             reads=["bones", bkk[7]], writes=[bk[6]])
                        S.op("act", lambda e: e.activation(out=g0[:], in_=banks[5][:, 0:TT], func=AF.Copy, scale=1.0 / 64),
                             reads=[bk[5]], writes=["gnt0"])
                        S.op("act", lambda e: e.activation(out=g1[:], in_=banks[5][:, 0:TT], func=AF.Square, scale=1.0 / 64),
                             reads=[bk[5]], writes=["gnt1"])
                        S.op("dve", lambda e: e.scalar_tensor_tensor(out=g1[:], in0=banks[6][:, 0:TT], scalar=1.0 / 64,
                                                                      in1=g1[:], op0=ALU.mult, op1=ALU.subtract),
                             reads=[bk[6], "gnt1"], writes=["gnt1"])
                        S.op("dve", lambda e: e.tensor_scalar(out=g1[:], in0=g1[:], scalar1=GN_EPS, scalar2=None,
                                                               op0=ALU.add), reads=["gnt1"], writes=["gnt1"])
                        S.op("act", lambda e: e.activation(out=g1[:], in_=g1[:], func=AF.Ln), reads=["gnt1"], writes=["gnt1"])
                        S.op("act", lambda e: e.activation(out=g1[:], in_=g1[:], func=AF.Exp, scale=-0.5),
                             reads=["gnt1"], writes=["gnt1"])
                        S.op("dve", lambda e: e.scalar_tensor_tensor(out=g0[:], in0=g0[:], scalar=-1.0, in1=g1[:],
                                                                      op0=ALU.mult, op1=ALU.mult),
                             reads=["gnt0", "gnt1"], writes=["gnt0"])
                        S.op("dve", lambda e: e.tensor_tensor(out=g1[:], in0=pyv, in1=g1[:], op=ALU.mult),
                             reads=[bk[7], "gnt1"], writes=["gnt1"])
                        S.op("dve", lambda e: e.tensor_tensor(out=g1[:], in0=g1[:], in1=g0[:], op=ALU.add),
                             reads=["gnt1", "gnt0"], writes=["gnt1"])
                        S.op("dve", lambda e: e.tensor_tensor(out=g1[:], in0=g1[:], in1=E1[:], op=ALU.mult),
                             reads=["gnt1", E1k[par]], writes=["gnt1"])
                        S.op("dve", lambda e, c=c: e.tensor_tensor(out=YG[:, c, :], in0=g1[:], in1=E2[:], op=ALU.add),
                             reads=["gnt1", E2k[par]], writes=["YG"])
                        if debug == 2:
                            S.op("act", lambda e: e.activation(out=g0[:], in_=pyv, func=AF.Copy), reads=[bk[7]], writes=["gnt0"])
                            S.dma("sp", dbg[s][c * 128:(c + 1) * 128, t0:t0 + TT], g0[:], reads=["gnt0"],
                                  writes=["dbgx%d" % S.ninst])
                        ck(9)

                    front(0, 0)
                    for c in range(NC):
                        nxt = deque()
                        if c + 1 < NC:
                            S.defer = nxt
                            front(c + 1, (c + 1) % 2)
                            S.defer = None
                        S.inter = (nxt, FRONT_RATIO)
                        S._acc = 0.0
                        S.npump = 0
                        scan(c, c % 2)
                        S.inter = None
                        S.pump(nxt, len(nxt))
                    if debug == 1:
                        S.dma("pool", dbg[s][:, t0:t0 + TT].rearrange("(c p) t -> p c t", p=128), YG[:],
                              reads=["YG"], writes=["dbg%d" % S.ninst])
                    S.link(["xs4", "xs5"], zkeys)
                    run_outproj_ln(lnbufs, W_AOUT, YG, "YG", xT[s][:, t0:t0 + TT], x1f[s][:, t0:t0 + TT],
                                   V_LNG0, V_LNB0, "A")

        ck(10)
        S.barrier()
        with ExitStack() as sb1:
            X1B = sb("X1B", [128, NC, TT], BF16, sb1)
            ostg = [sb("ostg%d" % i, [128, NC, TT], BF16, sb1) for i in range(2)]
            vstg = sb("vstg", [128, NB, C], BF16, sb1)
            vtmp = [sb("vtmp%d" % i, [128, TT], BF16, sb1) for i in range(2)]
            for s in range(NSEQ):
                for j in range(NT):
                    t0 = j * TT
                    S.dma("pool", X1B[:], x1f[s][:, t0:t0 + TT].rearrange("(c p) t -> p c t", p=128),
                          reads=["dstA"], writes=["X1B"])
                    for mi, m in enumerate((W_BK, W_BV, W_BQ, W_BG)):
                        og = ostg[mi % 2]
                        ogk = "ostg%d" % (mi % 2)
                        for c in range(NC):
                            wt, wkey = wnext(m, c)
                            pb = banks[c % 2][:, 0:TT]
                            pk = bk[c % 2]
                            fm_proj(pb, pk, wt, wkey, lambda dc: X1B[:, dc, :], "X1B")
                            if m == W_BG:
                                S.op("act", lambda e, og=og, c=c, pb=pb: e.activation(out=og[:, c, :], in_=pb, func=AF.Silu),
                                     reads=[pk], writes=[ogk])
                            elif m == W_BV:
                                i2 = c % 2
                                S.op("act", lambda e, i2=i2, pb=pb: e.activation(out=vtmp[i2][:], in_=pb, func=AF.Copy),
                                     reads=[pk], writes=["vtmp%d" % i2])
                                ptile = banks[2 + i2][:].bitcast(BF16)[:, 0:NB * 128].rearrange("p (a b) -> p a b", b=128)
                                for n in range(NB):
                                    S.op("pe", lambda e, i2=i2, n=n, ptile=ptile: e.transpose(
                                        ptile[:, n, :], vtmp[i2][:, n * 128:(n + 1) * 128], ident[:]),
                                        reads=["vtmp%d" % i2, "ident"], writes=[bk[2 + i2]])
                                S.op("dve", lambda e, c=c, ptile=ptile: e.tensor_copy(
                                    out=vstg[:, :, c * 128:(c + 1) * 128], in_=ptile),
                                    reads=[bk[2 + i2]], writes=["vstg"])
                            else:
                                S.op("act", lambda e, og=og, c=c, pb=pb: e.activation(out=og[:, c, :], in_=pb, func=AF.Copy),
                                     reads=[pk], writes=[ogk])
                        if m == W_BV:
                            S.dma("sp", VTd[s, j * NB:(j + 1) * NB].rearrange("n p c -> p n c"), vstg[:],
                                  reads=["vstg"], writes=["VTd%d" % S.ninst])
                        else:
                            dst = {W_BK: KTd, W_BQ: QTd, W_BG: SGd}[m]
                            S.dma("sp", dst[s][:, :, t0:t0 + TT].rearrange("c p t -> p c t"), og[:],
                                  reads=[ogk], writes=["scrB%d" % S.ninst])
        ck(11)
        S.barrier()
        with ExitStack() as sb2:
            lam_t = sb("lam_t", [128, 4, 128], F32, sb2)
            subg = sb("subg", [128, 256], F32, sb2)
            u4 = sb("u4", [128, HB, 512], F32, sb2)
            ud4 = sb("ud4", [128, HB, 512], F32, sb2)
            cbias = sb("cbias", [128, HB * 4], F32, sb2)
            lsc = sb("lsc", [128, 8], F32, sb2)
            ljunk = sb("ljunk", [128, 128], F32, sb2)
            S.dma("sp", lam_t[:].rearrange("p a b -> p (a b)"), lamd, writes=["lam_t"])
            S.dma("sp", subg[:], subgd, writes=["subg"])
            S.dma("sp", u4[:].rearrange("p a b -> p (a b)"), ualid, writes=["u4"])
            S.dma("sp", ud4[:].rearrange("p a b -> p (a b)"), dalid, writes=["ud4"])
            S.dma("sp", cbias[:], slpd, writes=["cbias"])
            for i in range(2):
                S.op("dve", lambda e, i=i: e.tensor_tensor(out=ljunk[:], in0=lam_t[:, 2 * i, :], in1=lam_t[:, 2 * i + 1, :],
                                                            op=ALU.mult), reads=["lam_t"], writes=["ljunk"])
                S.op("dve", lambda e, i=i: e.tensor_reduce(out=lsc[:, i:i + 1], in_=ljunk[:], axis=AX.X, op=ALU.add),
                     reads=["ljunk"], writes=["lsc"])
            S.op("act", lambda e: e.activation(out=lsc[:, 2:4], in_=lsc[:, 0:2], func=AF.Exp), reads=["lsc"], writes=["lsc"])
            S.op("dve", lambda e: e.tensor_tensor(out=lsc[:, 4:5], in0=lsc[:, 3:4], in1=lsc[:, 2:3], op=ALU.subtract),
                 reads=["lsc"], writes=["lsc"])
            S.op("dve", lambda e: e.tensor_scalar(out=lsc[:, 4:5], in0=lsc[:, 4:5], scalar1=-LAM_INIT, scalar2=None,
                                                   op0=ALU.add), reads=["lsc"], writes=["lsc"])
            S.op("dve", lambda e: e.tensor_scalar(out=subg[:], in0=subg[:], scalar1=1.0 - LAM_INIT, scalar2=None,
                                                   op0=ALU.mult), reads=["subg"], writes=["subg"])
            KTs = [sb("KT%d" % i, [128, 2, T], BF16, sb2) for i in range(2)]
            QTs = [sb("QT%d" % i, [128, 2, T], BF16, sb2) for i in range(2)]
            SGs = [sb("SG%d" % i, [128, 2, T], BF16, sb2) for i in range(2)]
            OGs = [sb("OG%d" % i, [128, 2, T], BF16, sb2) for i in range(2)]
            VTs = [sb("VT%d" % i, [128, NQB, 258], BF16, sb2) for i in range(2)]
            for i in range(2):
                S.op("dve", lambda e, i=i: e.memset(VTs[i][:, :, 256:258], 1.0), writes=["VT%d" % i])
            NR = 3
            stmp = [sb("stmp%d" % i, [128, 512], F32, sb2) for i in range(NR)]
            ptb = [sb("ptb%d" % i, [128, 512], BF16, sb2) for i in range(NR)]
            osb = [sb("osb%d" % i, [128, 256], F32, sb2) for i in range(2)]
            ojk = sb("ojk", [128, 256], F32, sb2)
            onb = [sb("onb%d" % i, [128, 256], BF16, sb2) for i in range(2)]
            rs = [sb("rs%d" % i, [128, 8], F32, sb2) for i in range(2)]
            scale = 128 ** -0.5
            LAG = 2

            def attn_head(s, h, ib):
                KT, QT, SG, OG, VT = KTs[ib], QTs[ib], SGs[ib], OGs[ib], VTs[ib]
                kK, kQ, kS, kO, kV = ("KT%d" % ib, "QT%d" % ib, "SG%d" % ib, "OG%d" % ib, "VT%d" % ib)
                S.dma("sp", KT[:], KTd[s, 2 * h:2 * h + 2].rearrange("c p t -> p c t"), writes=[kK])
                S.dma("sp", QT[:], QTd[s, 2 * h:2 * h + 2].rearrange("c p t -> p c t"), writes=[kQ])
                S.dma("sp", VT[:, :, 0:256], VTd[s][:, :, h * 256:(h + 1) * 256].rearrange("n p c -> p n c"), writes=[kV])
                S.dma("sp", SG[:], SGd[s, 2 * h:2 * h + 2].rearrange("c p t -> p c t"), writes=[kS])
                slope = 2.0 ** (-(8.0 / HB) * (h + 1))
                groups = []
                for qb in range(NQB):
                    for m in range(2):
                        ng = (qb + 4) // 4
                        for g in range(ng - 1, -1, -1):
                            hi = qb - 4 * g
                            lo = max(0, hi - 3)
                            groups.append((qb, m, g, lo, hi, g == ng - 1, g == 0))
                n = len(groups)

                def scores(i):
                    qb, m, g, lo, hi, first, last = groups[i]
                    r = i % NR
                    qs = slice(qb * 128, (qb + 1) * 128)
                    j0 = 3 - (hi - lo)
                    pst = banks[r]
                    for kb in range(lo, hi + 1):
                        j = j0 + (kb - lo)
                        S.op("pe", lambda e, pst=pst, m=m, kb=kb, qs=qs, j=j: e.matmul(
                            pst[:, j * 128:(j + 1) * 128], lhsT=KT[:, m, kb * 128:(kb + 1) * 128], rhs=QT[:, m, qs],
                            start=True, stop=True), reads=[kK, kQ], writes=[bk[r]])
                    bias_t = ud4 if g == 0 else u4
                    cs_ = slice(j0 * 128, 512)
                    S.op("dve", lambda e, pst=pst, r=r, bias_t=bias_t, cs_=cs_: e.scalar_tensor_tensor(
                        out=stmp[r][:, cs_], in0=pst[:, cs_], scalar=scale, in1=bias_t[:, h, cs_], op0=ALU.mult, op1=ALU.add),
                        reads=[bk[r], "u4", "ud4"], writes=["stmp%d" % r])
                    cst = -slope * 512.0 * g
                    S.op("act", lambda e, r=r, cst=cst, cs_=cs_: e.activation(
                        out=ptb[r][:, cs_], in_=stmp[r][:, cs_], func=AF.Exp, bias=cbias[:, h * 4 + g:h * 4 + g + 1]),
                        reads=["stmp%d" % r, "cbias"], writes=["ptb%d" % r])

                def pv(i):
                    qb, m, g, lo, hi, first, last = groups[i]
                    r = i % NR
                    par = qb % 2
                    j0 = 3 - (hi - lo)
                    pob = banks[4 + 2 * par + m]
                    pok = bk[4 + 2 * par + m]
                    for kb in range(lo, hi + 1):
                        j = j0 + (kb - lo)
                        S.op("pe", lambda e, pob=pob, r=r, kb=kb, j=j, st_=(first and kb == lo), sp_=(last and kb == hi):
                             e.matmul(pob[:, 0:257], lhsT=ptb[r][:, j * 128:(j + 1) * 128], rhs=VT[:, kb, 0:257],
                                      start=st_, stop=sp_), reads=["ptb%d" % r, kV], writes=[pok])
                    if last and m == 1:
                        S.pump(ep_q, len(ep_q))
                        S.defer = ep_q
                        epilogue(qb)
                        S.defer = None

                def epilogue(qb):
                    par = qb % 2
                    qs = slice(qb * 128, (qb + 1) * 128)
                    p0, p1 = banks[4 + 2 * par], banks[5 + 2 * par]
                    k0, k1 = bk[4 + 2 * par], bk[5 + 2 * par]
                    rs_, rk = rs[par], "rs%d" % par
                    ob, obk = osb[par], "osb%d" % par
                    nb_, nbk = onb[par], "onb%d" % par
                    S.op("dve", lambda e: e.reciprocal(out=rs_[:, 0:1], in_=p0[:, 256:257]), reads=[k0], writes=[rk])
                    S.op("dve", lambda e: e.reciprocal(out=rs_[:, 1:2], in_=p1[:, 256:257]), reads=[k1], writes=[rk])
                    S.op("dve", lambda e: e.tensor_tensor(out=rs_[:, 2:3], in0=rs_[:, 1:2], in1=lsc[:, 4:5], op=ALU.mult),
                         reads=[rk, "lsc"], writes=[rk])
                    S.op("dve", lambda e: e.tensor_scalar(out=ob[:], in0=p0[:, 0:256], scalar1=rs_[:, 0:1], scalar2=None,
                                                           op0=ALU.mult), reads=[k0, rk], writes=[obk])
                    S.op("dve", lambda e: e.scalar_tensor_tensor(out=ob[:], in0=p1[:, 0:256], scalar=rs_[:, 2:3],
                                                                  in1=ob[:], op0=ALU.mult, op1=ALU.add),
                         reads=[k1, rk, obk], writes=[obk])
                    S.op("act", lambda e: e.activation(out=ojk[:], in_=ob[:], func=AF.Square, accum_out=rs_[:, 3:4]),
                         reads=[obk], writes=["ojk", rk])
                    S.op("dve", lambda e: e.tensor_scalar(out=rs_[:, 4:5], in0=rs_[:, 3:4], scalar1=1.0 / 256,
                                                           scalar2=SUBLN_EPS, op0=ALU.mult, op1=ALU.add),
                         reads=[rk], writes=[rk])
                    S.op("act", lambda e: e.activation(out=rs_[:, 5:6], in_=rs_[:, 4:5], func=AF.Ln), reads=[rk], writes=[rk])
                    S.op("act", lambda e: e.activation(out=rs_[:, 6:7], in_=rs_[:, 5:6], func=AF.Exp, scale=-0.5),
                         reads=[rk], writes=[rk])
                    S.op("dve", lambda e: e.scalar_tensor_tensor(out=nb_[:], in0=ob[:], scalar=rs_[:, 6:7], in1=subg[:],
                                                                  op0=ALU.mult, op1=ALU.mult),
                         reads=[obk, rk, "subg"], writes=[nbk])
                    ptile = banks[3][:].bitcast(BF16)[:, 0:256].rearrange("p (a b) -> p a b", b=128)
                    for dv in range(2):
                        S.op("pe", lambda e, dv=dv: e.transpose(ptile[:, dv, :], nb_[:, dv * 128:(dv + 1) * 128], ident[:]),
                             reads=[nbk, "ident"], writes=[bk[3]])
                    S.op("dve", lambda e: e.tensor_tensor(out=OG[:, :, qs], in0=ptile, in1=SG[:, :, qs], op=ALU.mult),
                         reads=[bk[3], kS], writes=[kO])

                ep_q = deque()
                for i in range(n + LAG):
                    if i < n:
                        scores(i)
                        S.pump(ep_q, 3)
                    if i - LAG >= 0:
                        pv(i - LAG)
                S.pump(ep_q, len(ep_q))
                S.dma("sp", OGd[s, 2 * h:2 * h + 2].rearrange("c p t -> p c t"), OG[:], reads=[kO],
                      writes=["OGd%d" % S.ninst])

            ih = 0
            for s in range(NSEQ):
                for h in range(HB):
                    attn_head(s, h, ih % 2)
                    ih += 1
        ck(12)
        S.barrier()
        with ExitStack() as sb3:
            OGt = [sb("OGt%d" % i, [128, NC, TT], BF16, sb3) for i in range(2)]
            lnbs = [outproj_ln(sb3, "B0"), outproj_ln(sb3, "B1")]
            it = 0
            for s in range(NSEQ):
                for j in range(NT):
                    t0 = j * TT
                    og = OGt[it % 2]
                    ogk = "OGt%d" % (it % 2)
                    it += 1
                    S.dma("sp", og[:], OGd[s][:, :, t0:t0 + TT].rearrange("c p t -> p c t"), reads=["OGd"], writes=[ogk])
                    run_outproj_ln(lnbs[(it - 1) % 2], W_BOUT, og, ogk, x1f[s][:, t0:t0 + TT], outT[s][:, t0:t0 + TT],
                                   V_LNG1, V_LNB1, "B%d" % ((it - 1) % 2), boff=4 * ((it - 1) % 2))
        S.barrier()


def _consts(C, TT):
    HB = C // 256
    i = np.arange(64)
    mstrict = (i[:, None] < i[None, :]).astype(np.float32)
    mlow = (i[:, None] > i[None, :]).astype(np.float32)
    mincl = (i[:, None] <= i[None, :]).astype(np.float32)
    m5 = np.stack([np.stack([mm_] * 4, axis=0) for mm_ in (mstrict, mlow, mincl)], axis=0)
    m5 = np.ascontiguousarray(m5.transpose(2, 0, 1, 3)).reshape(64, 3 * 4 * 64)
    m5 = np.concatenate([m5, m5], axis=0)
    idm = np.concatenate([np.eye(64, dtype=np.float32)] * 4, axis=1)
    idm = np.concatenate([idm, idm], axis=0)
    bones = np.kron(np.eye(2, dtype=np.float32), np.ones((64, 64), np.float32))
    onesc = np.full((128, 128), 1.0 / C, np.float32)
    rmask = np.ones((128, TT), np.float32)
    rmask[:, ::L] = 0.0
    p = np.arange(128)
    ual = np.zeros((128, HB, 4, 128), np.float32)
    dal = np.zeros((128, HB, 4, 128), np.float32)
    cb = np.zeros((128, HB, 4), np.float32)
    allowed = (p[:, None] // 64) <= (p[None, :] // 64)
    for h in range(HB):
        slope = 2.0 ** (-(8.0 / HB) * (h + 1))
        for j in range(4):
            ual[:, h, j, :] = -slope * ((3 - j) * 128 + p[None, :] - p[:, None])
            dal[:, h, j, :] = ual[:, h, j, :]
            cb[:, h, j] = -slope * 512.0 * j
        dal[:, h, 3, :] = np.where(allowed, -slope * np.abs(p[None, :] - p[:, None]), -30000.0)
    return dict(identd=np.eye(128, dtype=np.float32), m5d=m5, idmd=idm, bonesd=bones, onescd=onesc,
                rmaskd=rmask, ualid=ual.reshape(128, HB * 512), dalid=dal.reshape(128, HB * 512),
                slpd=cb.reshape(128, HB * 4))


def _params(inp, C):
    NC = C // 128

    def fm_w(W):
        return np.ascontiguousarray(W.reshape(NC, 128, NC, 128).transpose(2, 1, 0, 3).reshape(NC, 128, NC * 128))

    def fm_v(V):
        n = V.shape[0]
        return np.ascontiguousarray(V.reshape(n, NC, 128).transpose(2, 0, 1).reshape(128, n * NC))

    f = lambda k: np.asarray(inp[k], np.float32)
    win = f("a_w_in")[0]
    wqg = f("b_w_qg")[0]
    mats = [win[0], win[1], win[2], win[3], f("a_w_out")[0], f("w_k_shared"), f("w_v_shared"),
            wqg[:, :C], wqg[:, C:], f("b_w_out")[0]]
    wall = np.stack([fm_w(m) for m in mats], axis=0)
    lora_dn = lambda W: np.ascontiguousarray(W.reshape(NC, 128, LORA).transpose(1, 0, 2).reshape(128, NC * LORA))
    vecs = np.stack([f("a_w0")[0], f("a_a0")[0], f("a_k_k")[0], f("a_k_a")[0], f("a_r_k")[0].reshape(C),
                     f("a_gn_g")[0], f("a_gn_b")[0], f("ln_g")[0], f("ln_b")[0], f("ln_g")[1], f("ln_b")[1]], axis=0)
    mu6 = np.concatenate([f("a_mu_proj")[0], f("a_mu_lora")[0]], axis=0)
    return dict(wall=wall, w1l=lora_dn(f("a_w1")[0]), a1l=lora_dn(f("a_a1")[0]),
                lw2d=np.ascontiguousarray(np.stack([f("a_w2")[0].reshape(LORA, NC, 128), f("a_a2")[0].reshape(LORA, NC, 128)],
                                                   axis=2).transpose(1, 0, 2, 3).reshape(NC, LORA, 256)),
                mud=fm_v(mu6), vecd=fm_v(vecs),
                lamd=np.ascontiguousarray(np.broadcast_to(f("b_lambda")[0].reshape(1, 512), (128, 512))),
                subgd=np.ascontiguousarray(np.broadcast_to(f("b_subln_g")[0].reshape(1, 256), (128, 256))))


def run(inp, n_cores, TT, debug=False, runner=None, stop=0):
    x = np.asarray(inp["x"], np.float32)
    B, T, C = x.shape
    NSEQ = B // n_cores
    nc = build(C, T, NSEQ, TT, debug=debug, stop=stop)
    common = dict(_consts(C, TT))
    common.update(_params(inp, C))
    in_maps = []
    for i in range(n_cores):
        d = dict(common)
        d["xT"] = np.ascontiguousarray(x[i * NSEQ:(i + 1) * NSEQ].transpose(0, 2, 1))
        in_maps.append(d)
    if runner is None:
        res = run_bass_kernel_spmd(nc, in_maps, core_ids=list(range(n_cores))).results
    else:
        res = runner(nc, in_maps)
    out = np.concatenate([np.asarray(r["outT"]).transpose(0, 2, 1) for r in res], axis=0)
    if debug:
        x1 = np.concatenate([np.asarray(r["x1f"]).transpose(0, 2, 1) for r in res], axis=0)
        dbg = np.concatenate([np.asarray(r["dbg"]).transpose(0, 2, 1) for r in res], axis=0)
        return np.ascontiguousarray(out), np.ascontiguousarray(x1), dbg
    return np.ascontiguousarray(out.astype(np.float32))


def kernel(**inputs):
    return run(inputs, 8, 512)
```
